# Optimizing a Trainium2 kernel written in Bass

```python
import jax, jax.numpy as jnp
from jax import lax
import numpy as np

D_MODEL = 1024
BATCH = 16
SEQ = 256
DEPTH = 4
DEC_BATCH = 8
DEC_SEQ = 4096
PAST_LEN = 256

GRID_W = 64
W_BR = 512
N_BRANCH = 3
H_A = 4
DK = 128
DV = 128
CONV_K = 3
DELTA_CHUNK = 64
G_B = 4
CH_B = W_BR // G_B
CHUNK_B = 128
G_C = 4
CH_C = W_BR // G_C
N_IN = 3 * W_BR + 4 * H_A + 6 * W_BR + N_BRANCH * D_MODEL
EPS = 1e-6

kernel_name = 'hybrid_delta_sgu_fourier_flow_step'


def rmsnorm(x, g):
    xf = x.astype(jnp.float32)
    y = xf * lax.rsqrt(jnp.mean(xf * xf, axis=-1, keepdims=True) + EPS)
    return (y * g.astype(jnp.float32)).astype(x.dtype)


def l2norm(x):
    return x * lax.rsqrt(jnp.sum(x * x, axis=-1, keepdims=True) + EPS)


def ada_mod(cvec, w, b):
    m = jnp.einsum('...d,de->...e', jax.nn.silu(cvec), w) + b
    return jnp.split(m, 3, axis=-1)


def short_conv(x, w):
    y = lax.conv_general_dilated(x, w[:, None, :].astype(x.dtype), window_strides=(1,),
                                 padding=((CONV_K // 2, CONV_K // 2),),
                                 dimension_numbers=('NWC', 'WIO', 'NWC'),
                                 feature_group_count=x.shape[-1])
    return jax.nn.silu(y)


def gated_delta_chunked(q, k, v, g, beta, s0):
    B, T, H, _ = q.shape
    n = T // DELTA_CHUNK

    def blk(t):
        t = t.reshape((B, n, DELTA_CHUNK) + t.shape[2:])
        return jnp.swapaxes(t, 2, 3)

    qc, kc, vc, gc, bc = blk(q), blk(k), blk(v), blk(g), blk(beta)
    gam = jnp.cumsum(gc, axis=-1)
    pos = jnp.arange(DELTA_CHUNK)
    strict = pos[:, None] > pos[None, :]
    incl = pos[:, None] >= pos[None, :]
    diff = gam[..., :, None] - gam[..., None, :]
    dec_strict = jnp.where(strict, jnp.exp(jnp.where(strict, diff, 0.0)), 0.0)
    dec_incl = jnp.where(incl, jnp.exp(jnp.where(incl, diff, 0.0)), 0.0)
    eye = jnp.eye(DELTA_CHUNK, dtype=q.dtype)
    m = eye + bc[..., :, None] * jnp.einsum('bnhik,bnhjk->bnhij', kc, kc) * dec_strict
    rhs = jnp.concatenate([vc * bc[..., None], kc * (bc * jnp.exp(gam))[..., None]], axis=-1)
    sol = lax.linalg.triangular_solve(m, rhs, left_side=True, lower=True, unit_diagonal=True)
    u0, w = sol[..., :DV], sol[..., DV:]
    qk = jnp.einsum('bnhik,bnhjk->bnhij', qc, kc) * dec_incl
    qg = qc * jnp.exp(gam)[..., None]
    kd = kc * jnp.exp(gam[..., -1:] - gam)[..., None]
    glast = jnp.exp(gam[..., -1])
    xs = tuple(jnp.moveaxis(t, 1, 0) for t in (u0, w, qk, qg, kd, glast))

    def step(s, inp):
        u0_, w_, qk_, qg_, kd_, gl_ = inp
        u = u0_ - jnp.einsum('bhck,bhkv->bhcv', w_, s)
        o = jnp.einsum('bhck,bhkv->bhcv', qg_, s) + jnp.einsum('bhij,bhjv->bhiv', qk_, u)
        s = gl_[..., None, None] * s + jnp.einsum('bhck,bhcv->bhkv', kd_, u)
        return s, o

    s_fin, o = lax.scan(step, s0, xs)
    o = jnp.transpose(o, (1, 0, 3, 2, 4)).reshape(B, T, H, DV)
    return o, s_fin


def delta_bidir(q, k, v, g, beta, s0_f, s0_b):
    o_f, s_f = gated_delta_chunked(q, k, v, g[:, :, 0], beta[:, :, 0], s0_f)
    rev = lambda t: jnp.flip(t, axis=1)
    o_b, s_b = gated_delta_chunked(rev(q), rev(k), rev(v), rev(g[:, :, 1]), rev(beta[:, :, 1]), s0_b)
    return o_f + rev(o_b), s_f, s_b


def chunk_sgu(u, v, g_norm, w_s, b_s):
    B, T, _ = u.shape
    n = T // CHUNK_B
    vn = rmsnorm(v, g_norm).reshape(B, n, CHUNK_B, G_B, CH_B)
    vs = jnp.einsum('gpq,bnqgc->bnpgc', w_s, vn) + b_s.T[None, None, :, :, None]
    return u * vs.reshape(B, T, W_BR)


def fourier_mix(xc, grid):
    B, T, _ = xc.shape
    xf = xc.astype(jnp.float32).reshape(B, T, G_C, CH_C)
    if grid is None:
        y = jnp.fft.fftn(xf, axes=(1, 3), norm='ortho').real
    else:
        rows, cols = grid
        y = jnp.fft.fftn(xf.reshape(B, rows, cols, G_C, CH_C), axes=(1, 2, 4), norm='ortho').real
    return y.reshape(B, T, W_BR).astype(xc.dtype)


def mixer(xn, w_in, conv_w, a_log, dt_bias, o_norm_g, sgu_norm_g, w_spatial, b_spatial,
          w_branch, w_out, s0_f, s0_b, grid):
    B, T, _ = xn.shape
    sizes = (3 * W_BR, 2 * H_A, 2 * H_A, W_BR, W_BR, W_BR, W_BR, W_BR, W_BR, N_BRANCH * D_MODEL)
    points = [int(p) for p in np.cumsum(sizes)[:-1]]
    proj = jnp.einsum('btd,de->bte', xn, w_in)
    qkv, beta, dec, z_a, u_b, v_b, z_b, x_c, z_c, gates = jnp.split(proj, points, axis=-1)

    qkv = short_conv(qkv, conv_w).astype(jnp.float32)
    q, k, v = jnp.split(qkv, 3, axis=-1)
    q = l2norm(q.reshape(B, T, H_A, DK)) * (DK ** -0.5)
    k = l2norm(k.reshape(B, T, H_A, DK))
    v = v.reshape(B, T, H_A, DV)
    beta = jax.nn.sigmoid(beta.astype(jnp.float32)).reshape(B, T, 2, H_A)
    g = -jnp.exp(a_log.astype(jnp.float32)) * jax.nn.softplus(
        dec.astype(jnp.float32).reshape(B, T, 2, H_A) + dt_bias.astype(jnp.float32))
    o, s_f, s_b = delta_bidir(q, k, v, g, beta, s0_f.astype(jnp.float32), s0_b.astype(jnp.float32))
    o = rmsnorm(o, o_norm_g) * jax.nn.silu(z_a.astype(jnp.float32).reshape(B, T, H_A, DV))
    o_a = o.reshape(B, T, W_BR).astype(xn.dtype)

    o_b = chunk_sgu(u_b, v_b, sgu_norm_g, w_spatial, b_spatial) * jax.nn.silu(z_b)

    o_c = fourier_mix(x_c, grid) * jax.nn.silu(z_c)

    br = jnp.stack([o_a, o_b, o_c], axis=2)
    pb = jnp.einsum('btnw,nwd->btnd', br, w_branch)
    gt = jax.nn.sigmoid(gates.reshape(B, T, N_BRANCH, D_MODEL))
    merged = jnp.einsum('btnd,btnd->btd', gt, pb)
    return jnp.einsum('btd,de->bte', merged, w_out), s_f, s_b


def setup_inputs(seed: int = 0) -> dict:
    key = jax.random.key(seed)
    ks = jax.random.split(key, 24)
    nrm = lambda k, s: jax.random.normal(k, s, jnp.float32)
    dt = jnp.exp(jax.random.uniform(ks[8], (DEPTH, 2, H_A), jnp.float32) * (np.log(0.1) - np.log(0.001)) + np.log(0.001))
    return {
        'x_prompt': nrm(ks[0], (BATCH, SEQ, D_MODEL)),
        'x_sample': nrm(ks[1], (DEC_BATCH, DEC_SEQ, D_MODEL)),
        'state_delta': 0.5 * nrm(ks[2], (DEC_BATCH, DEPTH, 2, H_A, DK, DV)),
        'c': nrm(ks[3], (DEC_BATCH, D_MODEL)),
        'c_ctx': nrm(ks[4], (D_MODEL,)),
        'ln_g': 1.0 + 0.05 * nrm(ks[5], (DEPTH, D_MODEL)),
        'w_ada': 0.5 * D_MODEL ** -0.5 * nrm(ks[6], (DEPTH, D_MODEL, 3 * D_MODEL)),
        'b_ada': 0.02 * nrm(ks[7], (DEPTH, 3 * D_MODEL)),
        'w_in': D_MODEL ** -0.5 * nrm(ks[9], (DEPTH, D_MODEL, N_IN)),
        'conv_w': CONV_K ** -0.5 * nrm(ks[10], (DEPTH, CONV_K, 3 * W_BR)),
        'a_log': jnp.log(jax.random.uniform(ks[11], (DEPTH, 2, H_A), jnp.float32, 1.0, 16.0)),
        'dt_bias': dt + jnp.log(-jnp.expm1(-dt)),
        'o_norm_g': 1.0 + 0.05 * nrm(ks[12], (DEPTH, DV)),
        'sgu_norm_g': 1.0 + 0.05 * nrm(ks[13], (DEPTH, W_BR)),
        'w_spatial': CHUNK_B ** -0.5 * nrm(ks[14], (DEPTH, G_B, CHUNK_B, CHUNK_B)),
        'b_spatial': 1.0 + 0.02 * nrm(ks[15], (DEPTH, G_B, CHUNK_B)),
        'w_branch': W_BR ** -0.5 * nrm(ks[16], (DEPTH, N_BRANCH, W_BR, D_MODEL)),
        'w_out': D_MODEL ** -0.5 * nrm(ks[17], (DEPTH, D_MODEL, D_MODEL)),
        'final_g': 1.0 + 0.05 * nrm(ks[18], (D_MODEL,)),
    }


def reference(x_prompt, x_sample, state_delta, c, c_ctx, ln_g, w_ada, b_ada, w_in, conv_w, a_log,
              dt_bias, o_norm_g, sgu_norm_g, w_spatial, b_spatial, w_branch, w_out, final_g):
    rows = x_sample.shape[1] // GRID_W

    def lp(l):
        return (w_in[l], conv_w[l], a_log[l], dt_bias[l], o_norm_g[l], sgu_norm_g[l],
                w_spatial[l], b_spatial[l], w_branch[l], w_out[l])

    h = x_prompt
    zeros = jnp.zeros((x_prompt.shape[0], H_A, DK, DV), jnp.float32)
    states = []
    for l in range(DEPTH):
        shift, scale, gate = ada_mod(c_ctx, w_ada[l], b_ada[l])
        xn = rmsnorm(h, ln_g[l]) * (1.0 + scale) + shift
        out, s_f, s_b = mixer(xn, *lp(l), zeros, zeros, None)
        h = h + gate * out
        states.append(jnp.stack([s_f, s_b], axis=1))
    y_prompt = rmsnorm(h, final_g)
    new_state_delta = jnp.stack(states, axis=1).astype(x_prompt.dtype)

    h = x_sample
    for l in range(DEPTH):
        shift, scale, gate = ada_mod(c, w_ada[l], b_ada[l])
        xn = rmsnorm(h, ln_g[l]) * (1.0 + scale[:, None, :]) + shift[:, None, :]
        out, _, _ = mixer(xn, *lp(l), state_delta[:, l, 0], state_delta[:, l, 1], (rows, GRID_W))
        h = h + gate[:, None, :] * out
    y_sample = rmsnorm(h, final_g)
    return (y_prompt, y_sample, new_state_delta)
```

```python
import os
from contextlib import ExitStack
import numpy as np
import concourse.bass as bass
import concourse.mybir as mybir
from concourse.bass_utils import run_bass_kernel_spmd

F32 = mybir.dt.float32
BF16 = mybir.dt.bfloat16
AF = mybir.ActivationFunctionType
ALU = mybir.AluOpType

D = 1024
DEPTH = 4
NS = 4096
NP = 512
NTOK = NS + NP
NIN = 7696
EPS = 1e-6
OFF = dict(qkv=0, beta=1536, dec=1544, za=1552, ub=2064, vb=2576, zb=3088, xc=3600, zc=4112, gates=4624)
SEGS = [(0, 4096), (4096, 256), (4352, 256)]

SEM_LIMIT = 30000
DBG_STAGE = float(os.environ.get('DBG_STAGE', '99'))
DBG_SEGS = os.environ.get('DBG_SEGS', '')
SAME_ENG_SYNC = {'pe': False, 'act': os.environ.get('SES','1')=='1', 'dve': os.environ.get('SES','1')=='1', 'pool': True, 'sp': False}


class Buf:
    __slots__ = ('name', 'lw', 'rd', 'dsem', 'dcnt')

    def __init__(self, name):
        self.name = name
        self.lw = None
        self.rd = []
        self.dsem = None
        self.dcnt = 0


class TB:
    __slots__ = ('t', 'b')

    def __init__(self, t, name):
        self.t = t
        self.b = Buf(name)

    def __getitem__(self, k):
        return self.t[k]


class Op:
    __slots__ = ('eng', 'idx', 'fn', 'waits', 'inc', 'sem', 'val', 'is_dma', 'owner', 'dval')


class SemPool:
    def __init__(self, nc):
        self.nc = nc
        self.free = []
        self.n = 0

    def get(self):
        if self.free:
            return self.free.pop()
        self.n += 1
        return [self.nc.alloc_semaphore(f"dq{self.n}"), 0]

    def put(self, ent):
        if ent[1] < SEM_LIMIT - 64:
            self.free.append(ent)


def _bufs(xs):
    return [x.b if isinstance(x, TB) else x for x in xs]


class Phase:
    ENG = ['pe', 'act', 'dve', 'pool', 'sp']

    def __init__(self, nc, pool):
        self.nc = nc
        self.pool = pool
        self.ops = {e: [] for e in self.ENG}
        self.synced = {e: {} for e in self.ENG}
        self.dma_sems = []

    def _add(self, eng, fn, R, W, is_dma=False, owner=None):
        R = _bufs(R)
        W = _bufs(W)
        o = Op()
        o.eng = eng
        o.idx = len(self.ops[eng])
        o.fn = fn
        o.inc = False
        o.is_dma = is_dma
        o.owner = owner
        o.waits = []
        deps = []
        for b in R:
            if b.lw is not None:
                deps.append(b.lw)
        for b in W:
            if b.lw is not None:
                deps.append(b.lw)
            deps.extend(b.rd)
        sy = self.synced[eng]
        for d in deps:
            if d.is_dma:
                ow = d.owner
                key = id(ow.dsem)
                val = ow.dsem[1]
                if sy.get(key, 0) >= val:
                    continue
                sy[key] = val
                o.waits.append(('d', ow.dsem[0], val))
            else:
                if d.eng == eng and not SAME_ENG_SYNC[eng]:
                    continue
                if sy.get(d.eng, -1) >= d.idx:
                    continue
                sy[d.eng] = d.idx
                d.inc = True
                o.waits.append(('e', d))
        for b in R:
            if not is_dma:
                b.rd = [r for r in b.rd if r.is_dma or r.eng != eng]
            b.rd.append(o)
        for b in W:
            b.lw = o
            b.rd = []
        self.ops[eng].append(o)
        return o

    def op(self, eng, fn, R=(), W=()):
        return self._add(eng, fn, R, W)

    def dma(self, out_ap, in_ap, R=(), W=(), owner=None, eng='sp', **kw):
        R = _bufs(R)
        W = _bufs(W)
        if owner is None:
            owner = W[0] if W else R[0]
        elif isinstance(owner, TB):
            owner = owner.b
        if owner.dsem is None:
            owner.dsem = self.pool.get()
            self.dma_sems.append(owner.dsem)
        nc = self.nc

        q = nc.sync if eng == 'sp' else nc.gpsimd

        def fn():
            return q.dma_start(out=out_ap, in_=in_ap, **kw)
        o = self._add(eng, fn, R, W, is_dma=True, owner=owner)
        owner.dsem[1] += 16
        o.dval = owner.dsem[1]
        o.sem = owner.dsem[0]
        return o

    def emit(self, eng_sems):
        nc = self.nc
        for e in self.ENG:
            if e == 'sp':
                continue
            for o in self.ops[e]:
                if o.inc:
                    cur = eng_sems[e][-1]
                    if cur[1] + 1 > SEM_LIMIT:
                        cur = [nc.alloc_semaphore(f"e_{e}_{len(eng_sems[e])}"), 0]
                        eng_sems[e].append(cur)
                    cur[1] += 1
                    o.sem = cur[0]
                    o.val = cur[1]
        ops = self.ops
        dma_sems = self.dma_sems

        def replay(e, h):
            for o in ops[e]:
                for w in o.waits:
                    if w[0] == 'd':
                        h.wait_ge(w[1], w[2])
                    else:
                        h.wait_ge(w[1].sem, w[1].val)
                ins = o.fn()
                if o.is_dma:
                    ins.then_inc(o.sem, 16)
                elif o.inc:
                    ins.then_inc(o.sem, 1)
            if e == 'sp':
                for s in dma_sems:
                    h.wait_ge(s[0], s[1])

        with nc.Block() as block:
            @block.tensor
            def _(h):
                replay('pe', h)

            @block.scalar
            def _(h):
                replay('act', h)

            @block.vector
            def _(h):
                replay('dve', h)

            @block.gpsimd
            def _(h):
                replay('pool', h)

            @block.sync
            def _(h):
                replay('sp', h)
        for s in dma_sems:
            self.pool.put(s)
        return {e: len(v) for e, v in self.ops.items()}


class K:
    def __init__(self, dbg=False, nlayers=DEPTH, stop_after=None):
        self.dbg = dbg
        self.nlayers = nlayers
        self.stop_after = stop_after
        nc = self.nc = bass.Bass("TRN2", target_bir_lowering=False)
        self.pool = SemPool(nc)
        self.eng_sems = {e: [[nc.alloc_semaphore(f"e_{e}_0"), 0]] for e in ['pe', 'act', 'dve', 'pool']}
        self.stats = []

        def di(name, shape, dt=F32):
            return nc.dram_tensor(name, list(shape), dt, kind="ExternalInput").ap()

        def do(name, shape, dt=F32):
            return nc.dram_tensor(name, list(shape), dt, kind="ExternalOutput").ap()

        def ds(name, shape, dt=F32):
            if dbg:
                return nc.dram_tensor(name, list(shape), dt, kind="ExternalOutput").ap()
            return nc.dram_tensor(name, list(shape), dt).ap()

        self.x_s = di("x_s", [NS, D])
        self.x_p = di("x_p", [NP, D])
        self.state = di("state", [DEPTH, 2, 4, 128, 128])
        self.c2 = di("c2", [16, 128])
        self.ln_g = di("ln_g", [32, 128])
        self.w_ada = di("w_ada", [DEPTH, D, 3 * D])
        self.b_ada = di("b_ada", [DEPTH, 24, 128])
        self.w_in = di("w_in", [DEPTH, D, NIN])
        self.conv_w = di("conv_w", [DEPTH, 36, 128])
        self.a_log = di("a_log", [DEPTH, 8])
        self.dt_bias = di("dt_bias", [DEPTH, 8])
        self.o_norm_g = di("o_norm_g", [DEPTH, 128])
        self.sgu_g = di("sgu_g", [DEPTH, 512])
        self.w_sp = di("w_sp", [DEPTH, 4, 128, 128])
        self.b_sp = di("b_sp", [DEPTH, 512])
        self.w_br = di("w_br", [DEPTH, 1536, D])
        self.w_out = di("w_out", [DEPTH, D, D])
        self.final_g = di("final_g", [D])
        self.c_ident = di("c_ident", [128, 128])
        self.c_tri = di("c_tri", [128, 6, 128])
        self.c_cs4 = di("c_cs4", [128, 512])
        self.c_bd = di("c_bd", [128, 2, 128])
        self.c_csr = di("c_csr", [128, 64])
        self.c_cp = di("c_cp", [128, 2, 2, 256])

        self.y_s = do("y_s", [NS, D])
        self.y_p = do("y_p", [NP, D])
        self.ns = do("ns", [2, DEPTH, 2, 4, 128, 128])

        self.hT = ds("hT", [D, NTOK])
        self.xnT = ds("xnT", [D, NTOK], BF16)
        self.qkvT = ds("qkvT", [1536, NTOK])
        self.gb = ds("gb", [NTOK, 16])
        self.PQ = ds("PQ", [NS, 4, 256], BF16)
        self.yT = ds("yT", [512, NTOK])
        self.oFB = [ds("oF", [NTOK, 512]), ds("oB", [NTOK, 512])]
        self.brT = ds("brT", [1536, NTOK], BF16)
        self.modv = ds("modv", [DEPTH, 128, 48])
        if dbg:
            self.dbgF = do("dbgF", [24, 128, 512])
            self.dbgB = do("dbgB", [24, 128, 512], BF16)

    def begin(self):
        self.es = ExitStack()
        self.ph = Phase(self.nc, self.pool)
        self._n = 0

    def end(self, name):
        with self.nc.named_scope(name):
            st = self.ph.emit(self.eng_sems)
        self.stats.append((name, st))
        self.es.close()

    def sb(self, shape, dt=F32, name=None):
        self._n += 1
        nm = f"{name or 't'}_{len(self.stats)}_{self._n}"
        return TB(self.es.enter_context(self.nc.sbuf_tensor(nm, list(shape), dt)), nm)

    def psum(self, shape, dt=F32):
        self._n += 1
        nm = f"ps_{len(self.stats)}_{self._n}"
        return TB(self.es.enter_context(self.nc.psum_tensor(nm, list(shape), dt)), nm)

    def mm(self, out, lhsT, rhs, R, W, start=True, stop=True):
        nc = self.nc
        self.ph.op('pe', lambda: nc.tensor.matmul(out, lhsT=lhsT, rhs=rhs, start=start, stop=stop), R, W)

    def tp(self, out, in_, ident, R, W):
        nc = self.nc
        self.ph.op('pe', lambda: nc.tensor.transpose(out, in_, ident), R, W)

    def act(self, out, in_, func, R, W, bias=None, scale=None, accum=None):
        nc = self.nc
        kw = {}
        if bias is not None:
            kw['bias'] = bias
        if scale is not None:
            kw['scale'] = scale
        if accum is not None:
            kw['accum_out'] = accum
        self.ph.op('act', lambda: nc.scalar.activation(out, in_, func, **kw), R, W)

    def _e(self, eng):
        return {'dve': self.nc.vector, 'pool': self.nc.gpsimd}[eng]

    def tt(self, eng, out, in0, in1, op, R, W):
        h = self._e(eng)
        self.ph.op(eng, lambda: h.tensor_tensor(out=out, in0=in0, in1=in1, op=op), R, W)

    def ts(self, eng, out, in0, s1, op0, R, W, s2=None, op1=None):
        h = self._e(eng)
        if op1 is None:
            self.ph.op(eng, lambda: h.tensor_scalar(out=out, in0=in0, scalar1=s1, scalar2=None, op0=op0), R, W)
        else:
            self.ph.op(eng, lambda: h.tensor_scalar(out=out, in0=in0, scalar1=s1, scalar2=s2, op0=op0, op1=op1), R, W)

    def stt(self, out, in0, scalar, in1, op0, op1, R, W):
        nc = self.nc
        self.ph.op('dve', lambda: nc.vector.scalar_tensor_tensor(out=out, in0=in0, scalar=scalar, in1=in1,
                                                                  op0=op0, op1=op1), R, W)

    def cp(self, eng, out, in_, R, W):
        nc = self.nc
        if eng == 'act':
            self.ph.op('act', lambda: nc.scalar.copy(out, in_), R, W)
        else:
            h = self._e(eng)
            self.ph.op(eng, lambda: h.tensor_copy(out=out, in_=in_), R, W)

    def ms(self, eng, ap, val, W):
        h = self._e(eng)
        self.ph.op(eng, lambda: h.memset(ap, val), (), W)

    def rcp(self, out, in_, R, W):
        nc = self.nc
        self.ph.op('dve', lambda: nc.vector.reciprocal(out, in_), R, W)

    def dma(self, out, in_, R=(), W=(), owner=None, eng='sp', **kw):
        self.ph.dma(out, in_, R, W, owner, eng, **kw)

    def load_T(self, dst_ap, dst_tb, src_rows, nrows, ident, ps):
        tmp = self.sb([nrows, 128], F32, "ldT")
        self.dma(tmp[:], src_rows, W=[tmp])
        self.tp(ps[:, 0:nrows], tmp[:], ident[0:nrows, 0:nrows], [tmp, ident], [ps])
        self.cp('dve', dst_ap, ps[:, 0:nrows], [ps], [dst_tb])

    def load_cast(self, dst, dcol0, src, nk, ncols, stg=None, eng_cycle=None):
        srcv = src.rearrange("(k p) n -> p k n", p=128)
        step = 512
        for c0 in range(0, ncols, step):
            w = min(step, ncols - c0)
            self.dma(dst[:, 0:nk, dcol0 + c0:dcol0 + c0 + w], srcv[:, :, c0:c0 + w], W=[dst], eng='pool')

    def phase_mod(self):
        nc = self.nc
        self.begin()
        ident = self.sb([128, 128]); self.dma(ident[:], self.c_ident, W=[ident])
        ps = self.psum([128, 512])
        ps2 = self.psum([128, 512])
        cT = self.sb([128, 16])
        self.load_T(cT[:], cT, self.c2, 16, ident, ps)
        scT = self.sb([128, 16])
        self.act(scT[:], cT[:], AF.Silu, [cT], [scT])
        lnT = self.sb([128, 32])
        self.load_T(lnT[:], lnT, self.ln_g, 32, ident, ps)
        wst = [self.sb([128, 8, 512], F32, "wst") for _ in range(2)]
        sc3 = scT[:].rearrange("p (g k) -> p k g", g=2)
        i = 0
        for l in range(self.nlayers):
            bT = self.sb([128, 24], F32, "bT")
            self.load_T(bT[:], bT, self.b_ada[l], 24, ident, ps)
            mod = self.sb([128, 24, 2], F32, "mod")
            for nb in range(6):
                w = wst[i % 2]
                i += 1
                self.dma(w[:], self.w_ada[l].rearrange("(k p) n -> p k n", p=128)[:, :, nb * 512:(nb + 1) * 512], W=[w])
                for jj in range(4):
                    j = nb * 4 + jj
                    for k in range(8):
                        self.mm(ps2[:, 2 * j:2 * j + 2], w[:, k, jj * 128:(jj + 1) * 128], sc3[:, k, :],
                                [w, scT], [ps2], start=(k == 0), stop=(k == 7))
            self.tt('dve', mod[:], ps2[:, 0:48].rearrange("p (j g) -> p j g", g=2),
                    bT[:].unsqueeze(2).to_broadcast([128, 24, 2]), ALU.add, [ps2, bT], [mod])
            gp = self.sb([128, 8, 2], F32, "gp")
            self.ts('dve', gp[:], mod[:, 8:16, :], 1.0, ALU.add, [mod], [gp])
            self.tt('dve', mod[:, 8:16, :], gp[:], lnT[:, l * 8:(l + 1) * 8].unsqueeze(2).to_broadcast([128, 8, 2]),
                    ALU.mult, [gp, lnT], [mod])
            self.dma(self.modv[l], mod[:].rearrange("p j g -> p (j g)"), R=[mod], owner=mod)
        self.end("mod")

    def phase_in(self):
        self.begin()
        ident = self.sb([128, 128]); self.dma(ident[:], self.c_ident, W=[ident])
        xin = [self.sb([128, 4, 1024], F32, "xin") for _ in range(3)]
        xT = [self.sb([128, 8, 512], F32, "xT") for _ in range(3)]
        ps = [self.psum([128, 512]) for _ in range(6)]
        hTv = self.hT.rearrange("(k p) t -> p k t", p=128)
        pi = 0
        for blk in range(NTOK // 512):
            s = blk % 3
            src = self.x_s[blk * 512:(blk + 1) * 512, :] if blk < 8 else self.x_p
            self.dma(xin[s][:], src.rearrange("(a p) d -> p a d", p=128), W=[xin[s]])
            for k in range(8):
                p = ps[pi % 6]
                pi += 1
                for a in range(4):
                    self.tp(p[:, a * 128:(a + 1) * 128], xin[s][:, a, k * 128:(k + 1) * 128], ident[:], [xin[s], ident], [p])
                self.cp('act' if k % 2 == 0 else 'dve', xT[s][:, k, :], p[:], [p], [xT[s]])
            self.dma(hTv[:, :, blk * 512:(blk + 1) * 512], xT[s][:], R=[xT[s]], owner=xT[s])
        self.end("in")

    def phase_out(self):
        self.begin()
        ident = self.sb([128, 128]); self.dma(ident[:], self.c_ident, W=[ident])
        fgb = self.sb([128, 1024]); self.dma(fgb[:], self.final_g.partition_broadcast(128), W=[fgb])
        hin = [self.sb([128, 8, 512], F32, "hin") for _ in range(3)]
        ht = [self.sb([128, 4, 1024], F32, "ht") for _ in range(3)]
        junk = self.sb([128, 1024])
        st = [self.sb([128, 4], F32, "st") for _ in range(4)]
        ps = [self.psum([128, 512]) for _ in range(6)]
        hTv = self.hT.rearrange("(k p) t -> p k t", p=128)
        pi = 0
        for blk in range(NTOK // 512):
            s = blk % 3
            self.dma(hin[s][:], hTv[:, :, blk * 512:(blk + 1) * 512], W=[hin[s]])
            for a in range(4):
                for half in range(2):
                    p = ps[pi % 6]
                    pi += 1
                    for j in range(4):
                        self.tp(p[:, j * 128:(j + 1) * 128], hin[s][:, half * 4 + j, a * 128:(a + 1) * 128], ident[:],
                                [hin[s], ident], [p])
                    self.cp('act' if half == 0 else 'dve', ht[s][:, a, half * 512:(half + 1) * 512], p[:], [p], [ht[s]])
                t_ = st[a]
                self.act(junk[:], ht[s][:, a, :], AF.Square, [ht[s]], [junk, t_], accum=t_[:, 0:1])
                self.act(t_[:, 1:2], t_[:, 0:1], AF.Sqrt, [t_], [t_], bias=EPS, scale=1.0 / D)
                self.rcp(t_[:, 2:3], t_[:, 1:2], [t_], [t_])
                self.stt(ht[s][:, a, :], ht[s][:, a, :], t_[:, 2:3], fgb[:], ALU.mult, ALU.mult, [ht[s], t_, fgb], [ht[s]])
            dst = self.y_s[blk * 512:(blk + 1) * 512, :] if blk < 8 else self.y_p
            self.dma(dst.rearrange("(a p) d -> p a d", p=128), ht[s][:], R=[ht[s]], owner=ht[s])
        self.end("out")

    def norm_block(self, hb, gi, modv, ones, sq, ps_ss, rtmp, rstd, tmps, xn):
        self.act(sq[:], hb[:], AF.Square, [hb], [sq])
        for k in range(8):
            self.mm(ps_ss[:], ones[:], sq[:, k, :], [ones, sq], [ps_ss], start=(k == 0), stop=(k == 7))
        self.act(rtmp[:], ps_ss[:], AF.Sqrt, [ps_ss], [rtmp], bias=EPS, scale=1.0 / D)
        self.rcp(rstd[:], rtmp[:], [rtmp], [rstd])
        for k in range(8):
            tm = tmps[k % 2]
            self.stt(tm[:], hb[:, k, :], modv[:, 16 + 2 * k + gi:16 + 2 * k + gi + 1], rstd[:], ALU.mult, ALU.mult,
                     [hb, modv, rstd], [tm])
            self.act(xn[:, k, :], tm[:], AF.Identity, [tm, modv], [xn], bias=modv[:, 2 * k + gi:2 * k + gi + 1], scale=1.0)

    def norm_block_g(self, hb, gi, modv, ones, sq, ps_ss, rtmp, rstd, tmps, xn):
        self.act(sq[:], hb[:], AF.Square, [hb], [sq])
        for k in range(8):
            self.mm(ps_ss[:], ones[:], sq[:, k, :], [ones, sq], [ps_ss], start=(k == 0), stop=(k == 7))
        self.act(rtmp[:], ps_ss[:], AF.Sqrt, [ps_ss], [rtmp], bias=EPS, scale=1.0 / D)
        self.rcp(rstd[:], rtmp[:], [rtmp], [rstd])
        yield
        for k in range(8):
            tm = tmps[k % 2]
            self.stt(tm[:], hb[:, k, :], modv[:, 16 + 2 * k + gi:16 + 2 * k + gi + 1], rstd[:], ALU.mult, ALU.mult,
                     [hb, modv, rstd], [tm])
            self.act(xn[:, k, :], tm[:], AF.Identity, [tm, modv], [xn], bias=modv[:, 2 * k + gi:2 * k + gi + 1], scale=1.0)
            if k % 2 == 1:
                yield


    def phase_p1(self, l):
        self.begin()
        ones = self.sb([128, 128], BF16); self.ms('pool', ones[:], 1.0, [ones])
        modv = self.sb([128, 48]); self.dma(modv[:], self.modv[l], W=[modv])
        adb = self.sb([128, 16])
        self.dma(adb[:, 0:8], self.a_log[l].partition_broadcast(128), W=[adb])
        self.dma(adb[:, 8:16], self.dt_bias[l].partition_broadcast(128), W=[adb])
        nea = self.sb([128, 8])
        self.act(nea[:], adb[:, 0:8], AF.Exp, [adb], [nea])
        self.ts('dve', nea[:], nea[:], -1.0, ALU.mult, [nea], [nea])
        w1 = self.sb([128, 8, 1552], BF16, "w1")
        stg = None
        self.load_cast(w1, 0, self.w_in[l][:, 0:1552], 8, 1552, stg)
        hb = [self.sb([128, 8, 512], F32, "hb") for _ in range(2)]
        xn = [self.sb([128, 8, 512], BF16, "xn") for _ in range(2)]
        sq = self.sb([128, 8, 512], BF16, "sq")
        rtmp = self.sb([128, 512]); rstd = self.sb([128, 512])
        tmps = [self.sb([128, 512], F32, "tm") for _ in range(2)]
        qst = [self.sb([128, 4, 512], F32, "qst") for _ in range(2)]
        gbt = [self.sb([128, 4, 16], F32, "gbt") for _ in range(2)]
        sm = [self.sb([128, 8], F32, "sm") for _ in range(4)]
        bdsb = self.sb([128, 64], F32, "bdsb")
        ps_ss = self.psum([128, 512])
        psq = [self.psum([128, 512]) for _ in range(4)]
        psb = self.psum([128, 512])
        hTv = self.hT.rearrange("(k p) t -> p k t", p=128)
        xnTv = self.xnT.rearrange("(k p) t -> p k t", p=128)
        qv = self.qkvT.rearrange("(j p) t -> p j t", p=128)
        cnt = [0]

        def sa(blk):
            s = blk % 2
            gi = 0 if blk < 8 else 1
            c0 = blk * 512
            self.dma(hb[s][:], hTv[:, :, c0:c0 + 512], W=[hb[s]])
            for _ in self.norm_block_g(hb[s], gi, modv, ones, sq, ps_ss, rtmp, rstd, tmps, xn[s]):
                yield
            self.dma(xnTv[:, :, c0:c0 + 512], xn[s][:], R=[xn[s]], owner=xn[s])

        def sb_(blk):
            s = blk % 2
            c0 = blk * 512
            for j in range(12):
                p = psq[cnt[0] % 4]
                cnt[0] += 1
                for k in range(8):
                    self.mm(p[:], w1[:, k, j * 128:(j + 1) * 128], xn[s][:, k, :], [w1, xn[s]], [p],
                            start=(k == 0), stop=(k == 7))
                q = qst[(j // 4) % 2]
                self.cp('act' if j % 2 == 0 else 'dve', q[:, j % 4, :], p[:], [p], [q])
                if j % 4 == 3:
                    self.dma(qv[:, j - 3:j + 1, c0:c0 + 512], q[:], R=[q], owner=q)
                yield
            g_ = gbt[s]
            for tl in range(4):
                for k in range(8):
                    self.mm(psb[:, tl * 16:(tl + 1) * 16], xn[s][:, k, tl * 128:(tl + 1) * 128], w1[:, k, 1536:1552],
                            [xn[s], w1], [psb], start=(k == 0), stop=(k == 7))
            self.cp('act', bdsb[:], psb[:, 0:64], [psb], [bdsb])
            yield
            for tl in range(4):
                pb_ = bdsb[:, tl * 16:tl * 16 + 8]
                pd_ = bdsb[:, tl * 16 + 8:tl * 16 + 16]
                self.act(sm[0][:], pb_, AF.Exp, [bdsb], [sm[0]], scale=-1.0)
                self.ts('dve', sm[0][:], sm[0][:], 1.0, ALU.add, [sm[0]], [sm[0]])
                self.rcp(g_[:, tl, 0:8], sm[0][:], [sm[0]], [g_])
                self.tt('dve', sm[1][:], pd_, adb[:, 8:16], ALU.add, [bdsb, adb], [sm[1]])
                self.stt(sm[2][:], sm[1][:], -1.0, sm[1][:], ALU.mult, ALU.max, [sm[1]], [sm[2]])
                self.act(sm[2][:], sm[2][:], AF.Exp, [sm[2]], [sm[2]], scale=-1.0)
                self.act(sm[2][:], sm[2][:], AF.Ln, [sm[2]], [sm[2]], bias=1.0, scale=1.0)
                self.stt(sm[3][:], sm[1][:], 0.0, sm[2][:], ALU.max, ALU.add, [sm[1], sm[2]], [sm[3]])
                self.tt('dve', g_[:, tl, 8:16], sm[3][:], nea[:], ALU.mult, [sm[3], nea], [g_])
                yield
            self.dma(self.gb[c0:c0 + 512, :].rearrange("(a p) c -> p a c", p=128), g_[:], R=[g_], owner=g_)

        nblk = NTOK // 512
        for i in range(nblk + 1):
            gs = []
            if i < nblk:
                gs.append(sa(i))
            if i >= 1:
                gs.append(sb_(i - 1))
            while gs:
                for g in list(gs):
                    try:
                        next(g)
                    except StopIteration:
                        gs.remove(g)
        self.end(f"p1_{l}")

    def phase_four(self, l):
        self.begin()
        stg = None
        wxc = self.sb([128, 8, 512], BF16, "wxc")
        self.load_cast(wxc, 0, self.w_in[l][:, OFF['xc']:OFF['xc'] + 512], 8, 512, stg)
        cf = self.sb([128, 512 + 256 + 1024]);
        self.dma(cf[:, 0:512], self.c_cs4, W=[cf])
        self.dma(cf[:, 512:768], self.c_bd.rearrange("p a b -> p (a b)"), W=[cf])
        self.dma(cf[:, 768:1792], self.c_cp.rearrange("p a b c -> p (a b c)"), W=[cf])
        cb = self.sb([128, 1792], BF16)
        self.cp('dve', cb[:], cf[:], [cf], [cb])
        cs4 = cb[:, 0:512]
        bdc = cb[:, 512:640]; bds = cb[:, 640:768]
        cpm = cb[:, 768:1792].rearrange("p (a b c) -> p a b c", a=2, b=2)
        xn = [self.sb([128, 8, 512], BF16, "xn") for _ in range(2)]
        xc2 = [self.sb([128, 4, 512], BF16, "xc") for _ in range(2)]
        ab = self.sb([128, 4, 4, 512], BF16, "ab")
        pq = [self.sb([128, 4, 4, 256], BF16, "pq") for _ in range(2)]
        ysb = [self.sb([128, 256], F32, "ysb") for _ in range(2)]
        psA = [self.psum([128, 512]) for _ in range(2)]
        ps = [self.psum([128, 512]) for _ in range(5)]
        xnTv = self.xnT.rearrange("(k p) t -> p k t", p=128)
        cnt = [0, 0, 0]

        def sa(blk):
            s = blk % 2
            c0 = blk * 512
            xc = xc2[s]
            self.dma(xn[s][:], xnTv[:, :, c0:c0 + 512], W=[xn[s]])
            for g in range(4):
                p = psA[cnt[0] % 2]; cnt[0] += 1
                for k in range(8):
                    self.mm(p[:], wxc[:, k, g * 128:(g + 1) * 128], xn[s][:, k, :], [wxc, xn[s]], [p],
                            start=(k == 0), stop=(k == 7))
                self.cp('act' if g % 2 == 0 else 'dve', xc[:, g, :], p[:], [p], [xc])
                yield

        def sb_(blk):
            s = blk % 2
            c0 = blk * 512
            xc = xc2[s]
            for tl in range(4):
                for g in range(4):
                    p = ps[cnt[1] % 5]; cnt[1] += 1
                    self.mm(p[:], xc[:, g, tl * 128:(tl + 1) * 128], cs4, [xc, cb], [p])
                    self.cp('act' if cnt[2] % 2 == 0 else 'dve', ab[:, tl, g, :], p[:], [p], [ab]); cnt[2] += 1
                yield
            if blk < 8:
                for tl in range(4):
                    for g2 in range(2):
                        p = ps[cnt[1] % 5]; cnt[1] += 1
                        for gg in range(2):
                            g = g2 * 2 + gg
                            self.mm(p[:, gg * 256:(gg + 1) * 256], bdc, ab[:, tl, g, 0:256], [cb, ab], [p], start=True, stop=False)
                            self.mm(p[:, gg * 256:(gg + 1) * 256], bds, ab[:, tl, g, 256:512], [cb, ab], [p], start=False, stop=True)
                        self.cp('act' if cnt[2] % 2 == 0 else 'dve', pq[s][:, tl, g2 * 2:g2 * 2 + 2, :],
                                p[:].rearrange("p (a b) -> p a b", a=2), [p], [pq[s]]); cnt[2] += 1
                    yield
                self.dma(self.PQ[c0:c0 + 512].rearrange("(a p) g c -> p a g c", p=128), pq[s][:], R=[pq[s]], owner=pq[s])
            else:
                for sq_ in range(2):
                    for g in range(4):
                        p = ps[cnt[1] % 5]; cnt[1] += 1
                        n = 0
                        for tc in range(2):
                            for m_ in range(2):
                                self.mm(p[:, 0:256], ab[:, sq_ * 2 + tc, g, m_ * 128:(m_ + 1) * 128], cpm[:, m_, tc, :],
                                        [ab, cb], [p], start=(n == 0), stop=(n == 3))
                                n += 1
                        y = ysb[cnt[2] % 2]
                        self.cp('act' if cnt[2] % 2 == 0 else 'dve', y[:], p[:, 0:256], [p], [y]); cnt[2] += 1
                        self.dma(self.yT[g * 128:(g + 1) * 128, NS + sq_ * 256:NS + (sq_ + 1) * 256], y[:], R=[y], owner=y)
                    yield

        nblk = NTOK // 512
        for i in range(nblk + 1):
            gs = []
            if i < nblk:
                gs.append(sa(i))
            if i >= 1:
                gs.append(sb_(i - 1))
            while gs:
                for g in list(gs):
                    try:
                        next(g)
                    except StopIteration:
                        gs.remove(g)
        self.end(f"four_{l}")

    def phase_four3(self, l):
        self.begin()
        cf = self.sb([128, 64]); self.dma(cf[:], self.c_csr, W=[cf])
        csr = self.sb([128, 64], BF16); self.cp('dve', csr[:], cf[:], [cf], [csr])
        pq2 = [self.sb([128, 64, 128], BF16, "pq2") for _ in range(2)]
        ysb = [self.sb([128, 64, 64], F32, "ysb") for _ in range(2)]
        ps = [self.psum([128, 8, 64]) for _ in range(4)]
        pqv = self.PQ.rearrange("(r c) g (q h) -> q r g c h", c=64, q=2)
        pi = 0
        for g in range(4):
            s = g % 2
            for q in range(2):
                self.dma(pq2[s][q * 64:(q + 1) * 64, :, :], pqv[q, :, g, :, :], W=[pq2[s]])
            for c8 in range(8):
                p = ps[pi % 4]; pi += 1
                for cc in range(8):
                    c = c8 * 8 + cc
                    self.mm(p[:, cc, :], pq2[s][:, c, :], csr[:], [pq2[s], csr], [p])
                self.cp('act' if c8 % 2 == 0 else 'dve', ysb[s][:, :, c8 * 8:(c8 + 1) * 8],
                        p[:].rearrange("p c r -> p r c"), [p], [ysb[s]])
            self.dma(self.yT[g * 128:(g + 1) * 128, 0:NS], ysb[s][:].rearrange("p r c -> p (r c)"), R=[ysb[s]], owner=ysb[s])
        self.end(f"four3_{l}")

    def phase_delta(self, l):
        self.begin()
        ident = self.sb([128, 128]); self.dma(ident[:], self.c_ident, W=[ident])
        identb = self.sb([128, 128], BF16); self.cp('dve', identb[:], ident[:], [ident], [identb])
        tri = self.sb([128, 6, 128]); self.dma(tri[:], self.c_tri, W=[tri])
        ones = self.sb([128, 128], BF16); self.ms('pool', ones[:], 1.0, [ones])
        cw = self.sb([128, 36])
        pS = [self.psum([128, 512]) for _ in range(2)]
        self.load_T(cw[:], cw, self.conv_w[l], 36, ident, pS[0])
        cst = dict(ident=ident, identb=identb, tri=tri, ones=ones, cw=cw)
        gens = [self._delta_stream(l, d, cst, pS[d]) for d in range(2)]
        active = list(gens)
        while active:
            for g in list(active):
                try:
                    next(g)
                except StopIteration:
                    active.remove(g)
        self.end(f"delta_{l}")

    def _delta_stream(self, l, d, cst, pS):
        ident, identb, tri, ones, cw = cst['ident'], cst['identb'], cst['tri'], cst['ones'], cst['cw']
        U = tri[:, 0 + d, :]
        mA = tri[:, 2 + d, :]
        mQ = tri[:, 0 + d, :]
        last = 127 if d == 0 else 0
        cwv = cw[:].rearrange("p (k j) -> p k j", k=3)
        two = lambda shape, dt=F32: [self.sb(shape, dt) for _ in range(2)]
        xin = two([128, 12, 130]); gbi = two([128, 16])
        c0t = self.sb([128, 12, 128]); c1t = self.sb([128, 12, 128])
        ybf2 = two([128, 8, 128], BF16); yv = self.sb([128, 4, 128])
        sq = self.sb([128, 8, 128], BF16)
        sc2 = two([128, 64])
        gbc = self.sb([128, 4, 128]); d1 = self.sb([128, 4, 128])
        Wt2 = two([128, 4, 128]); eG2 = two([128, 4, 128])
        WL = self.sb([128, 4, 128]); WU = self.sb([128, 4, 128])
        Ab = two([128, 4, 128]); Bb = two([128, 4, 128]); Pb = two([128, 4, 128])
        Af = self.sb([128, 4, 128]); Aoff = self.sb([128, 4, 128]); Td = self.sb([128, 4, 128]); M1 = self.sb([128, 4, 128])
        Pfin2 = two([128, 4, 128], BF16)
        mBD = tri[:, 4, :].unsqueeze(1).to_broadcast([128, 4, 128])
        mOFF = tri[:, 5, :].unsqueeze(1).to_broadcast([128, 4, 128])
        qkT2 = two([128, 4, 128], BF16); qgT2 = two([128, 4, 128], BF16)
        ktok2 = two([128, 4, 128], BF16); vbr2 = two([128, 4, 128])
        Yr = self.sb([128, 4, 128], BF16); usb = self.sb([128, 4, 128], BF16); ud = self.sb([128, 4, 128], BF16)
        osb = two([128, 4, 128])
        S = self.sb([128, 4, 128]); Sb = self.sb([128, 4, 128], BF16)
        pX = self.psum([128, 4, 128]); pY = self.psum([128, 4, 128]); pZ = self.psum([128, 4, 128])
        psm = pS[:, 256:512]
        pT = pS[:, 0:256].bitcast(BF16).rearrange("p (h v) -> p h v", h=4)
        SSQ, RQ, RK2, RK, GAM, EG, BRK, BRK2, C1N, T0 = [slice(i * 4, i * 4 + 4) for i in range(10)]
        I4 = ident[:].unsqueeze(1).to_broadcast([128, 4, 128])

        def bc(ap):
            return ap.unsqueeze(2).to_broadcast([128, 4, 128])

        qv = self.qkvT.rearrange("(j p) t -> p j t", p=128)

        def prep(ch, p):
            si, c, nch, t0 = ch
            x = xin[p]; gbx = gbi[p]; ybf = ybf2[p]; sc = sc2[p]; Wt = Wt2[p]; eG = eG2[p]
            qkT = qkT2[p]; qgT = qgT2[p]; ktok = ktok2[p]; vbr = vbr2[p]; Pfin = Pfin2[p]
            lo = 1 if c == 0 else 0
            hi = 129 if c == nch - 1 else 130
            if c == 0:
                self.ms('pool', x[:, :, 0:1], 0.0, [x])
            if c == nch - 1:
                self.ms('pool', x[:, :, 129:130], 0.0, [x])
            for jj in range(3):
                self.dma(x[:, jj * 4:(jj + 1) * 4, lo:hi], qv[:, jj * 4:(jj + 1) * 4, t0 - 1 + lo:t0 - 1 + hi], W=[x])
            self.dma(gbx[:], self.gb[t0:t0 + 128, :], W=[gbx])
            for k in range(3):
                wb = cwv[:, k, :].unsqueeze(2).to_broadcast([128, 12, 128])
                dst = c0t if k == 0 else c1t
                self.tt('pool', dst[:], x[:, :, k:k + 128], wb, ALU.mult, [x, cw], [dst])
                if k > 0:
                    self.tt('pool', c0t[:], c0t[:], c1t[:], ALU.add, [c0t, c1t], [c0t])
            self.act(ybf[:], c0t[:, 0:8, :], AF.Silu, [c0t], [ybf])
            self.act(yv[:], c0t[:, 8:12, :], AF.Silu, [c0t], [yv])
            yield
            self.tt('pool', sq[:], ybf[:], ybf[:], ALU.mult, [ybf], [sq])
            for j in range(8):
                self.mm(psm[:, j:j + 1], sq[:, j, :], ones[:, 0:1], [sq, ones], [pS])
            self.cp('act', sc[:, 48:56], psm[:, 0:8], [pS], [sc])
            self.act(sc[:, SSQ], sc[:, 48:52], AF.Sqrt, [sc], [sc], bias=128.0 * EPS, scale=128.0)
            self.rcp(sc[:, RQ], sc[:, SSQ], [sc], [sc])
            self.ts('dve', sc[:, T0], sc[:, 52:56], EPS, ALU.add, [sc], [sc])
            self.rcp(sc[:, RK2], sc[:, T0], [sc], [sc])
            self.act(sc[:, RK], sc[:, RK2], AF.Sqrt, [sc], [sc])
            b_ = gbx[:, d * 4:(d + 1) * 4]
            g_ = gbx[:, 8 + d * 4:8 + (d + 1) * 4]
            self.mm(psm[:, 16:20], U, g_, [tri, gbx], [pS])
            self.cp('dve', sc[:, GAM], psm[:, 16:20], [pS], [sc])
            self.act(sc[:, EG], sc[:, GAM], AF.Exp, [sc], [sc])
            self.tt('dve', sc[:, BRK], b_, sc[:, RK], ALU.mult, [gbx, sc], [sc])
            self.tt('dve', sc[:, BRK2], b_, sc[:, RK2], ALU.mult, [gbx, sc], [sc])
            self.stt(sc[:, C1N], sc[:, BRK2], -1.0, sc[:, EG], ALU.mult, ALU.mult, [sc], [sc])
            yield
            self.cp('dve', gbc[:], bc(g_), [gbx], [gbc])
            for h in range(4):
                self.mm(pX[:, h, :], gbc[:, h, :], U, [gbc, tri], [pX])
            self.tt('dve', d1[:], pX[:], bc(sc[:, GAM]), ALU.subtract, [pX, sc], [d1])
            self.stt(d1[:], d1[:], -1.0, d1[:], ALU.mult, ALU.min, [d1], [d1])
            self.act(Wt[:], d1[:], AF.Exp, [d1], [Wt])
            self.act(eG[:], pX[:], AF.Exp, [pX], [eG])
            self.tt('pool', WL[:], Wt[:], mA.unsqueeze(1).to_broadcast([128, 4, 128]), ALU.mult, [Wt, tri], [WL])
            self.tt('pool', WU[:], Wt[:], mQ.unsqueeze(1).to_broadcast([128, 4, 128]), ALU.mult, [Wt, tri], [WU])
            yield
            for h in range(4):
                self.mm(pX[:, h, :], ybf[:, 4 + h, :], ybf[:, 4 + h, :], [ybf], [pX])
                self.mm(pY[:, h, :], ybf[:, 4 + h, :], ybf[:, h, :], [ybf], [pY])
            A0 = Ab[0]
            for h in range(4):
                self.stt(Af[:, h, :], pX[:, h, :], sc[:, BRK2.start + h:BRK2.start + h + 1], WL[:, h, :],
                         ALU.mult, ALU.mult, [pX, sc, WL], [Af])
            self.tt('pool', A0[:], Af[:], mBD, ALU.mult, [Af, tri], [A0])
            self.tt('pool', Aoff[:], Af[:], mOFF, ALU.mult, [Af, tri], [Aoff])
            self.tt('dve', qkT[:], pY[:], WU[:], ALU.mult, [pY, WU], [qkT])
            self.tt('dve', qgT[:], ybf[:, 0:4, :], eG[:], ALU.mult, [ybf, eG], [qgT])
            yield
            for h in range(4):
                self.tp(pT[:, h, :], ybf[:, 4 + h, :], identb[:], [ybf, identb], [pS])
            self.cp('act', ktok[:], pT, [pS], [ktok])
            for h in range(4):
                self.tp(pX[:, h, :], yv[:, h, :], ident[:], [yv, ident], [pX])
            self.tt('dve', vbr[:], pX[:], bc(sc[:, BRK]), ALU.mult, [pX, sc], [vbr])
            yield
            for h in range(4):
                self.tp(pY[:, h, :], A0[:, h, :], ident[:], [A0, ident], [pY])
            B0 = Bb[0]
            self.cp('act', B0[:], pY[:], [pY], [B0])
            P0 = Pb[0]
            self.stt(P0[:], B0[:], -1.0, I4, ALU.mult, ALU.add, [B0, ident], [P0])
            yield
            cA, cB, cP = 0, 0, 0
            NL_ = 5
            for lvl in range(NL_):
                nA = Ab[1 - cA]; nB = Bb[1 - cB]
                for h in range(4):
                    self.mm(pX[:, h, :], Bb[cB][:, h, :], Ab[cA][:, h, :], [Bb[cB], Ab[cA]], [pX])
                if lvl < NL_ - 1:
                    for h in range(4):
                        self.mm(pY[:, h, :], Ab[cA][:, h, :], Bb[cB][:, h, :], [Bb[cB], Ab[cA]], [pY])
                self.cp('act', nA[:], pX[:], [pX], [nA])
                if lvl < NL_ - 1:
                    self.cp('act', nB[:], pY[:], [pY], [nB])
                yield
                for h in range(4):
                    self.mm(pX[:, h, :], nA[:, h, :], Pb[cP][:, h, :], [nA, Pb[cP]], [pX])
                self.tt('dve', Pb[1 - cP][:], pX[:], Pb[cP][:], ALU.add, [pX, Pb[cP]], [Pb[1 - cP]])
                cA, cB, cP = 1 - cA, 1 - cB, 1 - cP
                yield
            Pd = Pb[cP]
            for h in range(4):
                self.tp(pX[:, h, :], Pd[:, h, :], ident[:], [Pd, ident], [pX])
            self.cp('act', Td[:], pX[:], [pX], [Td])
            for h in range(4):
                self.mm(pY[:, h, :], Aoff[:, h, :], Pd[:, h, :], [Aoff, Pd], [pY])
            self.cp('act', M1[:], pY[:], [pY], [M1])
            yield
            for h in range(4):
                self.mm(pX[:, h, :], Td[:, h, :], M1[:, h, :], [Td, M1], [pX])
            self.tt('dve', Pfin[:], Pd[:], pX[:], ALU.subtract, [Pd, pX], [Pfin])
            yield

        def scan(ch, p):
            si, c, nch, t0 = ch
            ybf = ybf2[p]; sc = sc2[p]; Wt = Wt2[p]; eG = eG2[p]
            qkT = qkT2[p]; qgT = qgT2[p]; ktok = ktok2[p]; vbr = vbr2[p]; P = Pfin2[p]
            first = (c == 0) if d == 0 else (c == nch - 1)
            lastc = (c == nch - 1) if d == 0 else (c == 0)
            if first:
                if si == 0:
                    self.dma(S[:], self.state[l, d].rearrange("h k v -> k h v"), W=[S])
                else:
                    self.ms('pool', S[:], 0.0, [S])
                self.cp('act', Sb[:], S[:], [S], [Sb])
            for h in range(4):
                self.mm(pZ[:, h, :], ybf[:, 4 + h, :], Sb[:, h, :], [ybf, Sb], [pZ])
            for h in range(4):
                self.stt(Yr[:, h, :], pZ[:, h, :], sc[:, C1N.start + h:C1N.start + h + 1], vbr[:, h, :],
                         ALU.mult, ALU.add, [pZ, sc, vbr], [Yr])
            yield
            for h in range(4):
                self.mm(pZ[:, h, :], P[:, h, :], Yr[:, h, :], [P, Yr], [pZ])
            self.cp('dve', usb[:], pZ[:], [pZ], [usb])
            for h in range(4):
                self.ts('dve', ud[:, h, :], pZ[:, h, :], Wt[:, h, last:last + 1], ALU.mult, [pZ, Wt], [ud])
            yield
            for h in range(4):
                self.mm(pZ[:, h, :], qgT[:, h, :], Sb[:, h, :], [qgT, Sb], [pZ], start=True, stop=False)
                self.mm(pZ[:, h, :], qkT[:, h, :], usb[:, h, :], [qkT, usb], [pZ], start=False, stop=True)
            o_ = osb[p]
            self.tt('dve', o_[:], pZ[:], bc(sc[:, RQ]), ALU.mult, [pZ, sc], [o_])
            self.dma(self.oFB[d][t0:t0 + 128, :], o_[:].rearrange("p h v -> p (h v)"), R=[o_], owner=o_)
            yield
            for h in range(4):
                self.mm(pZ[:, h, :], ktok[:, h, :], ud[:, h, :], [ktok, ud], [pZ])
            for h in range(4):
                self.stt(S[:, h, :], S[:, h, :], eG[:, h, last:last + 1], pZ[:, h, :], ALU.mult, ALU.add,
                         [S, eG, pZ], [S])
            self.cp('act', Sb[:], S[:], [S], [Sb])
            if lastc and si > 0:
                self.dma(self.ns[si - 1, l, d].rearrange("h k v -> k h v"), S[:], R=[S], owner=S)
            yield

        chunks = []
        for si, (seg0, segT) in enumerate(SEGS):
            nch = segT // 128
            order = range(nch) if d == 0 else range(nch - 1, -1, -1)
            for c in order:
                chunks.append((si, c, nch, seg0 + c * 128))
        for i in range(len(chunks) + 1):
            gs = []
            if i < len(chunks):
                gs.append(prep(chunks[i], i % 2))
            if i >= 1:
                gs.append(scan(chunks[i - 1], (i - 1) % 2))
            while gs:
                for g in list(gs):
                    try:
                        next(g)
                    except StopIteration:
                        gs.remove(g)
                        continue
                    yield

    def phase_p5a(self, l):
        self.begin()
        ident = self.sb([128, 128]); self.dma(ident[:], self.c_ident, W=[ident])
        identb = self.sb([128, 128], BF16); self.cp('dve', identb[:], ident[:], [ident], [identb])
        stg = None
        w5 = self.sb([128, 8, 2560], BF16, "w5")
        self.load_cast(w5, 0, self.w_in[l][:, OFF['za']:OFF['za'] + 2048], 8, 2048, stg)
        self.load_cast(w5, 2048, self.w_in[l][:, OFF['zc']:OFF['zc'] + 512], 8, 512, stg)
        ZA, UB, VB, ZB, ZC = 0, 512, 1024, 1536, 2048
        ongb = self.sb([128, 4, 128])
        for h in range(4):
            self.dma(ongb[:, h, :], self.o_norm_g[l].partition_broadcast(128), W=[ongb])
        sgb = self.sb([128, 512]); self.dma(sgb[:], self.sgu_g[l].partition_broadcast(128), W=[sgb])
        bsb = self.sb([128, 4, 128]); self.dma(bsb[:].rearrange("p g q -> p (g q)"), self.b_sp[l].partition_broadcast(128), W=[bsb])
        wsf = self.sb([128, 4, 128]); self.dma(wsf[:], self.w_sp[l].rearrange("g p q -> p g q"), W=[wsf])
        wsT = self.sb([128, 4, 128], BF16)
        psx = self.psum([128, 4, 128])
        for g in range(4):
            self.tp(psx[:, g, :], wsf[:, g, :], ident[:], [wsf, ident], [psx])
        self.cp('dve', wsT[:], psx[:], [psx], [wsT])

        xn = [self.sb([128, 8, 512], BF16, "xn") for _ in range(2)]
        oF = [self.sb([128, 4, 512], F32, "oF") for _ in range(2)]
        oB = [self.sb([128, 4, 512], F32, "oB") for _ in range(2)]
        yt = [self.sb([128, 4, 512], F32, "yt") for _ in range(2)]
        br = [self.sb([128, 12, 512], BF16, "br") for _ in range(2)]
        ubT = self.sb([128, 4, 512]); szb = self.sb([128, 4, 512])
        uz2 = [self.sb([128, 4, 512], F32, "uz") for _ in range(2)]
        szc = [self.sb([128, 512], F32, "szc") for _ in range(2)]
        zas = self.sb([128, 512]); osum = self.sb([128, 512]); t1 = self.sb([128, 512])
        junkA = self.sb([128, 128]); junkB = self.sb([128, 512])
        oab = self.sb([128, 512], BF16)
        vbs = self.sb([128, 512]); vn = self.sb([128, 512], BF16); vs = self.sb([128, 4, 128])
        stA = self.sb([128, 16]); stB = self.sb([128, 16])
        ps = [self.psum([128, 512]) for _ in range(5)]
        pT = self.psum([128, 8, 128], BF16)
        xnTv = self.xnT.rearrange("(k p) t -> p k t", p=128)
        brv = self.brT.rearrange("(j p) t -> p j t", p=128)
        yTv = self.yT.rearrange("(g p) t -> p g t", p=128)
        cnt = [0]

        def proj_fm(p, col0, x):
            for k in range(8):
                self.mm(p[:], w5[:, k, col0:col0 + 128], x[:, k, :], [w5, x], [p], start=(k == 0), stop=(k == 7))

        def proj_tm(p, col0, x, tl):
            for k in range(8):
                self.mm(p[:], x[:, k, tl * 128:(tl + 1) * 128], w5[:, k, col0:col0 + 512], [w5, x], [p],
                        start=(k == 0), stop=(k == 7))

        def fm(blk):
            s = blk % 2
            c0 = blk * 512
            x = xn[s]; b_ = br[s]; uz = uz2[s]
            self.dma(x[:], xnTv[:, :, c0:c0 + 512], W=[x])
            self.dma(oF[s][:], self.oFB[0][c0:c0 + 512, :].rearrange("(a p) c -> p a c", p=128), W=[oF[s]])
            self.dma(oB[s][:], self.oFB[1][c0:c0 + 512, :].rearrange("(a p) c -> p a c", p=128), W=[oB[s]])
            self.dma(yt[s][:], yTv[:, :, c0:c0 + 512], W=[yt[s]])
            for g in range(4):
                p = ps[cnt[0] % 3]; cnt[0] += 1
                proj_fm(p, UB + g * 128, x)
                self.cp('dve', ubT[:, g, :], p[:], [p], [ubT])
                yield
                p = ps[cnt[0] % 3]; cnt[0] += 1
                proj_fm(p, ZB + g * 128, x)
                self.act(szb[:, g, :], p[:], AF.Silu, [p], [szb])
                yield
            self.tt('pool', uz[:], ubT[:], szb[:], ALU.mult, [ubT, szb], [uz])
            for g in range(4):
                p = ps[cnt[0] % 3]; cnt[0] += 1
                proj_fm(p, ZC + g * 128, x)
                z = szc[g % 2]
                self.act(z[:], p[:], AF.Silu, [p], [z])
                self.tt('dve', b_[:, 8 + g, :], z[:], yt[s][:, g, :], ALU.mult, [z, yt[s]], [b_])
                yield

        def ta(blk):
            s = blk % 2
            x = xn[s]; b_ = br[s]
            p = ps[3]
            for tl in range(4):
                tsl = slice(tl * 128, (tl + 1) * 128)
                proj_tm(p, ZA, x, tl)
                self.act(zas[:], p[:], AF.Silu, [p], [zas])
                self.tt('dve', osum[:], oF[s][:, tl, :], oB[s][:, tl, :], ALU.add, [oF[s], oB[s]], [osum])
                yield
                for h in range(4):
                    self.act(junkA[:], osum[:, h * 128:(h + 1) * 128], AF.Square, [osum], [junkA, stA],
                             accum=stA[:, h:h + 1])
                self.act(stA[:, 4:8], stA[:, 0:4], AF.Sqrt, [stA], [stA], bias=EPS, scale=1.0 / 128)
                self.rcp(stA[:, 8:12], stA[:, 4:8], [stA], [stA])
                yield
                self.tt('dve', t1[:].rearrange("p (h v) -> p h v", h=4), osum[:].rearrange("p (h v) -> p h v", h=4),
                        stA[:, 8:12].unsqueeze(2).to_broadcast([128, 4, 128]), ALU.mult, [osum, stA], [t1])
                self.tt('pool', t1[:], t1[:], ongb[:].rearrange("p h v -> p (h v)"), ALU.mult, [t1, ongb], [t1])
                self.tt('dve', oab[:], t1[:], zas[:], ALU.mult, [t1, zas], [oab])
                yield
                for h in range(4):
                    self.tp(pT[:, h, :], oab[:, h * 128:(h + 1) * 128], identb[:], [oab, identb], [pT])
                self.cp('act', b_[:, 0:4, tsl], pT[:, 0:4, :], [pT], [b_])
                yield

        def tb(blk):
            s = blk % 2
            x = xn[s]; b_ = br[s]; uz = uz2[s]
            p = ps[4]
            for tl in range(4):
                tsl = slice(tl * 128, (tl + 1) * 128)
                proj_tm(p, VB, x, tl)
                self.cp('act', vbs[:], p[:], [p], [vbs])
                yield
                self.act(junkB[:], vbs[:], AF.Square, [vbs], [junkB, stB], accum=stB[:, 12:13])
                self.act(stB[:, 13:14], stB[:, 12:13], AF.Sqrt, [stB], [stB], bias=EPS, scale=1.0 / 512)
                self.rcp(stB[:, 14:15], stB[:, 13:14], [stB], [stB])
                self.stt(vn[:], vbs[:], stB[:, 14:15], sgb[:], ALU.mult, ALU.mult, [vbs, stB, sgb], [vn])
                yield
                for g in range(4):
                    self.mm(psx[:, g, :], vn[:, g * 128:(g + 1) * 128], wsT[:, g, :], [vn, wsT], [psx])
                self.tt('dve', vs[:], psx[:], bsb[:], ALU.add, [psx, bsb], [vs])
                self.tt('dve', b_[:, 4:8, tsl], vs[:], uz[:, :, tsl], ALU.mult, [vs, uz], [b_])
                yield

        nblk = NTOK // 512
        for i in range(nblk + 1):
            gs = []
            if i < nblk:
                gs.append(fm(i))
            if i >= 1:
                gs += [ta(i - 1), tb(i - 1)]
            while gs:
                for g in list(gs):
                    try:
                        next(g)
                    except StopIteration:
                        gs.remove(g)
            if i >= 1:
                b_ = br[(i - 1) % 2]
                self.dma(brv[:, :, (i - 1) * 512:i * 512], b_[:], R=[b_], owner=b_)
        self.end(f"p5a_{l}")

    def phase_p5b(self, l):
        self.begin()
        modv = self.sb([128, 48]); self.dma(modv[:], self.modv[l], W=[modv])
        stg = None
        stg12 = None
        wg = self.sb([128, 8, 3072], BF16, "wg")
        wbr = self.sb([128, 12, 1024], BF16, "wbr")
        wo = self.sb([128, 8, 1024], BF16, "wo")
        wgB = [Buf(f"wg{i}") for i in range(6)]
        wbrB = [Buf(f"wbr{i}") for i in range(2)]
        woB = [Buf(f"wo{i}") for i in range(2)]
        gsrc = self.w_in[l][:, OFF['gates']:OFF['gates'] + 3072].rearrange("(k p) n -> p k n", p=128)
        bsrc = self.w_br[l].rearrange("(k p) n -> p k n", p=128)
        osrc = self.w_out[l].rearrange("(k p) n -> p k n", p=128)
        for h in range(2):
            for n in range(3):
                c = n * 1024 + h * 512
                self.dma(wg[:, :, c:c + 512], gsrc[:, :, c:c + 512], W=[wgB[n * 2 + h]], eng='pool')
            self.dma(wbr[:, :, h * 512:(h + 1) * 512], bsrc[:, :, h * 512:(h + 1) * 512], W=[wbrB[h]], eng='pool')
        for h in range(2):
            self.dma(wo[:, :, h * 512:(h + 1) * 512], osrc[:, :, h * 512:(h + 1) * 512], W=[woB[h]], eng='pool')
        xn = [self.sb([128, 8, 512], BF16, "xn") for _ in range(2)]
        br = [self.sb([128, 12, 512], BF16, "br") for _ in range(2)]
        hb = [self.sb([128, 8, 512], F32, "hb") for _ in range(2)]
        mg = self.sb([128, 8, 512], BF16, "mg")
        sig = [self.sb([128, 512], F32, "sig") for _ in range(2)]
        macc = self.sb([128, 512]); mt = self.sb([128, 512])
        psg = [self.psum([128, 512]) for _ in range(3)]
        psp = [self.psum([128, 512]) for _ in range(3)]
        pso = [self.psum([128, 512]) for _ in range(2)]
        xnTv = self.xnT.rearrange("(k p) t -> p k t", p=128)
        brv = self.brT.rearrange("(j p) t -> p j t", p=128)
        hTv = self.hT.rearrange("(k p) t -> p k t", p=128)
        gi_, pi_, si_ = 0, 0, 0
        for blk in range(NTOK // 512):
            s = blk % 2
            gi = 0 if blk < 8 else 1
            c0 = blk * 512
            x = xn[s]; b_ = br[s]; h_ = hb[s]
            self.dma(x[:], xnTv[:, :, c0:c0 + 512], W=[x])
            self.dma(b_[:], brv[:, :, c0:c0 + 512], W=[b_])
            self.dma(h_[:], hTv[:, :, c0:c0 + 512], W=[h_])
            for j in range(8):
                for n in range(3):
                    pg = psg[gi_ % 3]; gi_ += 1
                    for k in range(8):
                        self.mm(pg[:], wg[:, k, n * 1024 + j * 128:n * 1024 + (j + 1) * 128], x[:, k, :], [wgB[n * 2 + j // 4], x], [pg],
                                start=(k == 0), stop=(k == 7))
                    sg = sig[si_ % 2]; si_ += 1
                    self.act(sg[:], pg[:], AF.Sigmoid, [pg], [sg])
                    pp = psp[pi_ % 3]; pi_ += 1
                    for wc in range(4):
                        self.mm(pp[:], wbr[:, n * 4 + wc, j * 128:(j + 1) * 128], b_[:, n * 4 + wc, :], [wbrB[j // 4], b_], [pp],
                                start=(wc == 0), stop=(wc == 3))
                    if n == 0:
                        self.tt('dve', macc[:], sg[:], pp[:], ALU.mult, [sg, pp], [macc])
                    elif n == 1:
                        self.tt('dve', mt[:], sg[:], pp[:], ALU.mult, [sg, pp], [mt])
                        self.tt('pool', macc[:], macc[:], mt[:], ALU.add, [macc, mt], [macc])
                    else:
                        self.tt('dve', mt[:], sg[:], pp[:], ALU.mult, [sg, pp], [mt])
                        self.tt('dve', mg[:, j, :], macc[:], mt[:], ALU.add, [macc, mt], [mg])
            for e in range(8):
                po = pso[e % 2]
                for j in range(8):
                    self.mm(po[:], wo[:, j, e * 128:(e + 1) * 128], mg[:, j, :], [woB[e // 4], mg], [po], start=(j == 0), stop=(j == 7))
                self.stt(h_[:, e, :], po[:], modv[:, 32 + 2 * e + gi:32 + 2 * e + gi + 1], h_[:, e, :], ALU.mult, ALU.add,
                         [po, modv, h_], [h_])
            self.dma(hTv[:, :, c0:c0 + 512], h_[:], R=[h_], owner=h_)
        self.end(f"p5b_{l}")

    def build(self):
        seq = [("mod", self.phase_mod), ("in", self.phase_in)]
        for l in range(self.nlayers):
            seq += [(f"p1_{l}", lambda l=l: self.phase_p1(l)),
                    (f"four_{l}", lambda l=l: self.phase_four(l)),
                    (f"four3_{l}", lambda l=l: self.phase_four3(l)),
                    (f"delta_{l}", lambda l=l: self.phase_delta(l)),
                    (f"p5a_{l}", lambda l=l: self.phase_p5a(l)),
                    (f"p5b_{l}", lambda l=l: self.phase_p5b(l))]
        seq += [("out", self.phase_out)]
        for name, fn in seq:
            fn()
            if self.stop_after == name:
                break
        return self.nc


def host_consts():
    i = np.arange(128)
    ident = np.eye(128, dtype=np.float32)
    tri = np.zeros((128, 6, 128), np.float32)
    tri[:, 0, :] = (i[:, None] <= i[None, :])
    tri[:, 1, :] = (i[:, None] >= i[None, :])
    tri[:, 2, :] = (i[:, None] > i[None, :])
    tri[:, 3, :] = (i[:, None] < i[None, :])
    tri[:, 4, :] = ((i[:, None] // 64) == (i[None, :] // 64))
    tri[:, 5, :] = 1.0 - tri[:, 4, :]

    def cs(n):
        a = np.arange(n)
        th = 2 * np.pi * np.outer(a, a) / n
        return (np.cos(th) / np.sqrt(n)), (np.sin(th) / np.sqrt(n))
    c128, s128 = cs(128)
    cs4 = np.concatenate([c128, s128, -s128, c128], axis=1).astype(np.float32)
    c64, s64 = cs(64)
    bd = np.zeros((128, 2, 128), np.float32)
    for r in range(2):
        bd[r * 64:(r + 1) * 64, 0, r * 64:(r + 1) * 64] = c64
        bd[r * 64:(r + 1) * 64, 1, r * 64:(r + 1) * 64] = s64
    csr = np.concatenate([c64, -s64], axis=0).astype(np.float32)
    c256, s256 = cs(256)
    cp = np.zeros((128, 2, 2, 256), np.float32)
    for tc in range(2):
        cp[:, 0, tc, :] = c256[tc * 128:(tc + 1) * 128, :]
        cp[:, 1, tc, :] = -s256[tc * 128:(tc + 1) * 128, :]
    return dict(c_ident=ident, c_tri=tri, c_cs4=cs4, c_bd=bd, c_csr=csr, c_cp=cp)


def make_in_maps(x_prompt, x_sample, state_delta, c, c_ctx, ln_g, w_ada, b_ada, w_in, conv_w, a_log,
                 dt_bias, o_norm_g, sgu_norm_g, w_spatial, b_spatial, w_branch, w_out, final_g):
    f = lambda a: np.ascontiguousarray(np.asarray(a, dtype=np.float32))
    consts = host_consts()
    shared = dict(
        ln_g=f(ln_g).reshape(32, 128), w_ada=f(w_ada), b_ada=f(b_ada).reshape(DEPTH, 24, 128), w_in=f(w_in),
        conv_w=f(conv_w).reshape(DEPTH, 36, 128), a_log=f(a_log).reshape(DEPTH, 8), dt_bias=f(dt_bias).reshape(DEPTH, 8),
        o_norm_g=f(o_norm_g), sgu_g=f(sgu_norm_g), w_sp=f(w_spatial), b_sp=f(b_spatial).reshape(DEPTH, 512),
        w_br=f(w_branch).reshape(DEPTH, 1536, D), w_out=f(w_out), final_g=f(final_g), **consts)
    x_prompt = f(x_prompt); x_sample = f(x_sample); state_delta = f(state_delta); c = f(c); c_ctx = f(c_ctx)
    maps = []
    for i in range(8):
        m = dict(shared)
        m["x_s"] = x_sample[i]
        m["x_p"] = np.ascontiguousarray(x_prompt[2 * i:2 * i + 2].reshape(NP, D))
        m["state"] = state_delta[i]
        m["c2"] = np.ascontiguousarray(np.stack([c[i], c_ctx], 0).reshape(16, 128))
        maps.append(m)
    return maps


def kernel(**inputs):
    nc = K().build()
    maps = make_in_maps(**inputs)
    res = run_bass_kernel_spmd(nc, maps, core_ids=list(range(8)))
    r = res.results
    y_sample = np.stack([r[i]["y_s"] for i in range(8)], 0)
    y_prompt = np.concatenate([r[i]["y_p"].reshape(2, 256, D) for i in range(8)], 0)
    ns = np.concatenate([r[i]["ns"] for i in range(8)], 0)
    return (y_prompt.astype(np.float32), y_sample.astype(np.float32), ns.astype(np.float32))
```

```python
import os
from contextlib import ExitStack
import numpy as np
import concourse.bass as bass
import concourse.mybir as mybir
from concourse.bass_utils import run_bass_kernel_spmd

F32 = mybir.dt.float32
BF16 = mybir.dt.bfloat16
AF = mybir.ActivationFunctionType
ALU = mybir.AluOpType

D = 1024
DEPTH = 4
NS = 4096
NP = 512
NTOK = NS + NP
NIN = 7696
EPS = 1e-6
OFF = dict(qkv=0, beta=1536, dec=1544, za=1552, ub=2064, vb=2576, zb=3088, xc=3600, zc=4112, gates=4624)
SEGS = [(0, 4096), (4096, 256), (4352, 256)]

SEM_LIMIT = 30000
DBG_STAGE = float(os.environ.get('DBG_STAGE', '99'))
DBG_SEGS = os.environ.get('DBG_SEGS', '')
SAME_ENG_SYNC = {'pe': False, 'act': os.environ.get('SES','1')=='1', 'dve': os.environ.get('SES','1')=='1', 'pool': True, 'sp': False}


class Buf:
    __slots__ = ('name', 'lw', 'rd', 'dsem', 'dcnt')

    def __init__(self, name):
        self.name = name
        self.lw = None
        self.rd = []
        self.dsem = None
        self.dcnt = 0


class TB:
    __slots__ = ('t', 'b')

    def __init__(self, t, name):
        self.t = t
        self.b = Buf(name)

    def __getitem__(self, k):
        return self.t[k]


class Op:
    __slots__ = ('eng', 'idx', 'fn', 'waits', 'inc', 'sem', 'val', 'is_dma', 'owner', 'dval')


class SemPool:
    def __init__(self, nc):
        self.nc = nc
        self.free = []
        self.n = 0

    def get(self):
        if self.free:
            return self.free.pop()
        self.n += 1
        return [self.nc.alloc_semaphore(f"dq{self.n}"), 0]

    def put(self, ent):
        if ent[1] < SEM_LIMIT - 64:
            self.free.append(ent)


def _bufs(xs):
    return [x.b if isinstance(x, TB) else x for x in xs]


class Phase:
    ENG = ['pe', 'act', 'dve', 'pool', 'sp']

    def __init__(self, nc, pool):
        self.nc = nc
        self.pool = pool
        self.ops = {e: [] for e in self.ENG}
        self.synced = {e: {} for e in self.ENG}
        self.dma_sems = []

    def _add(self, eng, fn, R, W, is_dma=False, owner=None):
        R = _bufs(R)
        W = _bufs(W)
        o = Op()
        o.eng = eng
        o.idx = len(self.ops[eng])
        o.fn = fn
        o.inc = False
        o.is_dma = is_dma
        o.owner = owner
        o.waits = []
        deps = []
        for b in R:
            if b.lw is not None:
                deps.append(b.lw)
        for b in W:
            if b.lw is not None:
                deps.append(b.lw)
            deps.extend(b.rd)
        sy = self.synced[eng]
        for d in deps:
            if d.is_dma:
                ow = d.owner
                key = id(ow.dsem)
                val = ow.dsem[1]
                if sy.get(key, 0) >= val:
                    continue
                sy[key] = val
                o.waits.append(('d', ow.dsem[0], val))
            else:
                if d.eng == eng and not SAME_ENG_SYNC[eng]:
                    continue
                if sy.get(d.eng, -1) >= d.idx:
                    continue
                sy[d.eng] = d.idx
                d.inc = True
                o.waits.append(('e', d))
        for b in R:
            if not is_dma:
                b.rd = [r for r in b.rd if r.is_dma or r.eng != eng]
            b.rd.append(o)
        for b in W:
            b.lw = o
            b.rd = []
        self.ops[eng].append(o)
        return o

    def op(self, eng, fn, R=(), W=()):
        return self._add(eng, fn, R, W)

    def dma(self, out_ap, in_ap, R=(), W=(), owner=None, eng='sp', **kw):
        R = _bufs(R)
        W = _bufs(W)
        if owner is None:
            owner = W[0] if W else R[0]
        elif isinstance(owner, TB):
            owner = owner.b
        if owner.dsem is None:
            owner.dsem = self.pool.get()
            self.dma_sems.append(owner.dsem)
        nc = self.nc

        q = nc.sync if eng == 'sp' else nc.gpsimd

        def fn():
            return q.dma_start(out=out_ap, in_=in_ap, **kw)
        o = self._add(eng, fn, R, W, is_dma=True, owner=owner)
        owner.dsem[1] += 16
        o.dval = owner.dsem[1]
        o.sem = owner.dsem[0]
        return o

    def emit(self, eng_sems):
        nc = self.nc
        for e in self.ENG:
            if e == 'sp':
                continue
            for o in self.ops[e]:
                if o.inc:
                    cur = eng_sems[e][-1]
                    if cur[1] + 1 > SEM_LIMIT:
                        cur = [nc.alloc_semaphore(f"e_{e}_{len(eng_sems[e])}"), 0]
                        eng_sems[e].append(cur)
                    cur[1] += 1
                    o.sem = cur[0]
                    o.val = cur[1]
        ops = self.ops
        dma_sems = self.dma_sems

        def replay(e, h):
            for o in ops[e]:
                for w in o.waits:
                    if w[0] == 'd':
                        h.wait_ge(w[1], w[2])
                    else:
                        h.wait_ge(w[1].sem, w[1].val)
                ins = o.fn()
                if o.is_dma:
                    ins.then_inc(o.sem, 16)
                elif o.inc:
                    ins.then_inc(o.sem, 1)
            if e == 'sp':
                for s in dma_sems:
                    h.wait_ge(s[0], s[1])

        with nc.Block() as block:
            @block.tensor
            def _(h):
                replay('pe', h)

            @block.scalar
            def _(h):
                replay('act', h)

            @block.vector
            def _(h):
                replay('dve', h)

            @block.gpsimd
            def _(h):
                replay('pool', h)

            @block.sync
            def _(h):
                replay('sp', h)
        for s in dma_sems:
            self.pool.put(s)
        return {e: len(v) for e, v in self.ops.items()}


class K:
    def __init__(self, dbg=False, nlayers=DEPTH, stop_after=None):
        self.dbg = dbg
        self.nlayers = nlayers
        self.stop_after = stop_after
        nc = self.nc = bass.Bass("TRN2", target_bir_lowering=False)
        self.pool = SemPool(nc)
        self.eng_sems = {e: [[nc.alloc_semaphore(f"e_{e}_0"), 0]] for e in ['pe', 'act', 'dve', 'pool']}
        self.stats = []

        def di(name, shape, dt=F32):
            return nc.dram_tensor(name, list(shape), dt, kind="ExternalInput").ap()

        def do(name, shape, dt=F32):
            return nc.dram_tensor(name, list(shape), dt, kind="ExternalOutput").ap()

        def ds(name, shape, dt=F32):
            if dbg:
                return nc.dram_tensor(name, list(shape), dt, kind="ExternalOutput").ap()
            return nc.dram_tensor(name, list(shape), dt).ap()

        self.x_s = di("x_s", [NS, D])
        self.x_p = di("x_p", [NP, D])
        self.state = di("state", [DEPTH, 2, 4, 128, 128])
        self.c2 = di("c2", [16, 128])
        self.ln_g = di("ln_g", [32, 128])
        self.w_ada = di("w_ada", [DEPTH, D, 3 * D])
        self.b_ada = di("b_ada", [DEPTH, 24, 128])
        self.w_in = di("w_in", [DEPTH, D, NIN])
        self.conv_w = di("conv_w", [DEPTH, 36, 128])
        self.a_log = di("a_log", [DEPTH, 8])
        self.dt_bias = di("dt_bias", [DEPTH, 8])
        self.o_norm_g = di("o_norm_g", [DEPTH, 128])
        self.sgu_g = di("sgu_g", [DEPTH, 512])
        self.w_sp = di("w_sp", [DEPTH, 4, 128, 128])
        self.b_sp = di("b_sp", [DEPTH, 512])
        self.w_br = di("w_br", [DEPTH, 1536, D])
        self.w_out = di("w_out", [DEPTH, D, D])
        self.final_g = di("final_g", [D])
        self.c_ident = di("c_ident", [128, 128])
        self.c_tri = di("c_tri", [128, 6, 128])
        self.c_cs4 = di("c_cs4", [128, 512])
        self.c_bd = di("c_bd", [128, 2, 128])
        self.c_csr = di("c_csr", [128, 64])
        self.c_cp = di("c_cp", [128, 2, 2, 256])

        self.y_s = do("y_s", [NS, D])
        self.y_p = do("y_p", [NP, D])
        self.ns = do("ns", [2, DEPTH, 2, 4, 128, 128])

        self.hT = ds("hT", [D, NTOK])
        self.xnT = ds("xnT", [D, NTOK], BF16)
        self.qkvT = ds("qkvT", [1536, NTOK])
        self.gb = ds("gb", [NTOK, 16])
        self.PQ = ds("PQ", [NS, 4, 256], BF16)
        self.yT = ds("yT", [512, NTOK])
        self.oFB = [ds("oF", [NTOK, 512]), ds("oB", [NTOK, 512])]
        self.brT = ds("brT", [1536, NTOK], BF16)
        self.modv = ds("modv", [DEPTH, 128, 48])
        if dbg:
            self.dbgF = do("dbgF", [24, 128, 512])
            self.dbgB = do("dbgB", [24, 128, 512], BF16)

    def begin(self):
        self.es = ExitStack()
        self.ph = Phase(self.nc, self.pool)
        self._n = 0

    def end(self, name):
        with self.nc.named_scope(name):
            st = self.ph.emit(self.eng_sems)
        self.stats.append((name, st))
        self.es.close()

    def sb(self, shape, dt=F32, name=None):
        self._n += 1
        nm = f"{name or 't'}_{len(self.stats)}_{self._n}"
        return TB(self.es.enter_context(self.nc.sbuf_tensor(nm, list(shape), dt)), nm)

    def psum(self, shape, dt=F32):
        self._n += 1
        nm = f"ps_{len(self.stats)}_{self._n}"
        return TB(self.es.enter_context(self.nc.psum_tensor(nm, list(shape), dt)), nm)

    def mm(self, out, lhsT, rhs, R, W, start=True, stop=True):
        nc = self.nc
        self.ph.op('pe', lambda: nc.tensor.matmul(out, lhsT=lhsT, rhs=rhs, start=start, stop=stop), R, W)

    def tp(self, out, in_, ident, R, W):
        nc = self.nc
        self.ph.op('pe', lambda: nc.tensor.transpose(out, in_, ident), R, W)

    def act(self, out, in_, func, R, W, bias=None, scale=None, accum=None):
        nc = self.nc
        kw = {}
        if bias is not None:
            kw['bias'] = bias
        if scale is not None:
            kw['scale'] = scale
        if accum is not None:
            kw['accum_out'] = accum
        self.ph.op('act', lambda: nc.scalar.activation(out, in_, func, **kw), R, W)

    def _e(self, eng):
        return {'dve': self.nc.vector, 'pool': self.nc.gpsimd}[eng]

    def tt(self, eng, out, in0, in1, op, R, W):
        h = self._e(eng)
        self.ph.op(eng, lambda: h.tensor_tensor(out=out, in0=in0, in1=in1, op=op), R, W)

    def ts(self, eng, out, in0, s1, op0, R, W, s2=None, op1=None):
        h = self._e(eng)
        if op1 is None:
            self.ph.op(eng, lambda: h.tensor_scalar(out=out, in0=in0, scalar1=s1, scalar2=None, op0=op0), R, W)
        else:
            self.ph.op(eng, lambda: h.tensor_scalar(out=out, in0=in0, scalar1=s1, scalar2=s2, op0=op0, op1=op1), R, W)

    def stt(self, out, in0, scalar, in1, op0, op1, R, W):
        nc = self.nc
        self.ph.op('dve', lambda: nc.vector.scalar_tensor_tensor(out=out, in0=in0, scalar=scalar, in1=in1,
                                                                  op0=op0, op1=op1), R, W)

    def cp(self, eng, out, in_, R, W):
        nc = self.nc
        if eng == 'act':
            self.ph.op('act', lambda: nc.scalar.copy(out, in_), R, W)
        else:
            h = self._e(eng)
            self.ph.op(eng, lambda: h.tensor_copy(out=out, in_=in_), R, W)

    def ms(self, eng, ap, val, W):
        h = self._e(eng)
        self.ph.op(eng, lambda: h.memset(ap, val), (), W)

    def rcp(self, out, in_, R, W):
        nc = self.nc
        self.ph.op('dve', lambda: nc.vector.reciprocal(out, in_), R, W)

    def dma(self, out, in_, R=(), W=(), owner=None, eng='sp', **kw):
        self.ph.dma(out, in_, R, W, owner, eng, **kw)

    def load_T(self, dst_ap, dst_tb, src_rows, nrows, ident, ps):
        tmp = self.sb([nrows, 128], F32, "ldT")
        self.dma(tmp[:], src_rows, W=[tmp])
        self.tp(ps[:, 0:nrows], tmp[:], ident[0:nrows, 0:nrows], [tmp, ident], [ps])
        self.cp('dve', dst_ap, ps[:, 0:nrows], [ps], [dst_tb])

    def load_cast(self, dst, dcol0, src, nk, ncols, stg=None, eng_cycle=None):
        srcv = src.rearrange("(k p) n -> p k n", p=128)
        step = 512
        for c0 in range(0, ncols, step):
            w = min(step, ncols - c0)
            self.dma(dst[:, 0:nk, dcol0 + c0:dcol0 + c0 + w], srcv[:, :, c0:c0 + w], W=[dst], eng='pool')

    def phase_mod(self):
        nc = self.nc
        self.begin()
        ident = self.sb([128, 128]); self.dma(ident[:], self.c_ident, W=[ident])
        ps = self.psum([128, 512])
        ps2 = self.psum([128, 512])
        cT = self.sb([128, 16])
        self.load_T(cT[:], cT, self.c2, 16, ident, ps)
        scT = self.sb([128, 16])
        self.act(scT[:], cT[:], AF.Silu, [cT], [scT])
        lnT = self.sb([128, 32])
        self.load_T(lnT[:], lnT, self.ln_g, 32, ident, ps)
        wst = [self.sb([128, 8, 512], F32, "wst") for _ in range(2)]
        sc3 = scT[:].rearrange("p (g k) -> p k g", g=2)
        i = 0
        for l in range(self.nlayers):
            bT = self.sb([128, 24], F32, "bT")
            self.load_T(bT[:], bT, self.b_ada[l], 24, ident, ps)
            mod = self.sb([128, 24, 2], F32, "mod")
            for nb in range(6):
                w = wst[i % 2]
                i += 1
                self.dma(w[:], self.w_ada[l].rearrange("(k p) n -> p k n", p=128)[:, :, nb * 512:(nb + 1) * 512], W=[w])
                for jj in range(4):
                    j = nb * 4 + jj
                    for k in range(8):
                        self.mm(ps2[:, 2 * j:2 * j + 2], w[:, k, jj * 128:(jj + 1) * 128], sc3[:, k, :],
                                [w, scT], [ps2], start=(k == 0), stop=(k == 7))
            self.tt('dve', mod[:], ps2[:, 0:48].rearrange("p (j g) -> p j g", g=2),
                    bT[:].unsqueeze(2).to_broadcast([128, 24, 2]), ALU.add, [ps2, bT], [mod])
            gp = self.sb([128, 8, 2], F32, "gp")
            self.ts('dve', gp[:], mod[:, 8:16, :], 1.0, ALU.add, [mod], [gp])
            self.tt('dve', mod[:, 8:16, :], gp[:], lnT[:, l * 8:(l + 1) * 8].unsqueeze(2).to_broadcast([128, 8, 2]),
                    ALU.mult, [gp, lnT], [mod])
            self.dma(self.modv[l], mod[:].rearrange("p j g -> p (j g)"), R=[mod], owner=mod)
        self.end("mod")

    def phase_in(self):
        self.begin()
        ident = self.sb([128, 128]); self.dma(ident[:], self.c_ident, W=[ident])
        xin = [self.sb([128, 4, 1024], F32, "xin") for _ in range(3)]
        xT = [self.sb([128, 8, 512], F32, "xT") for _ in range(3)]
        ps = [self.psum([128, 512]) for _ in range(6)]
        hTv = self.hT.rearrange("(k p) t -> p k t", p=128)
        pi = 0
        for blk in range(NTOK // 512):
            s = blk % 3
            src = self.x_s[blk * 512:(blk + 1) * 512, :] if blk < 8 else self.x_p
            self.dma(xin[s][:], src.rearrange("(a p) d -> p a d", p=128), W=[xin[s]])
            for k in range(8):
                p = ps[pi % 6]
                pi += 1
                for a in range(4):
                    self.tp(p[:, a * 128:(a + 1) * 128], xin[s][:, a, k * 128:(k + 1) * 128], ident[:], [xin[s], ident], [p])
                self.cp('act' if k % 2 == 0 else 'dve', xT[s][:, k, :], p[:], [p], [xT[s]])
            self.dma(hTv[:, :, blk * 512:(blk + 1) * 512], xT[s][:], R=[xT[s]], owner=xT[s])
        self.end("in")

    def phase_out(self):
        self.begin()
        ident = self.sb([128, 128]); self.dma(ident[:], self.c_ident, W=[ident])
        fgb = self.sb([128, 1024]); self.dma(fgb[:], self.final_g.partition_broadcast(128), W=[fgb])
        hin = [self.sb([128, 8, 512], F32, "hin") for _ in range(3)]
        ht = [self.sb([128, 4, 1024], F32, "ht") for _ in range(3)]
        junk = self.sb([128, 1024])
        st = [self.sb([128, 4], F32, "st") for _ in range(4)]
        ps = [self.psum([128, 512]) for _ in range(6)]
        hTv = self.hT.rearrange("(k p) t -> p k t", p=128)
        pi = 0
        for blk in range(NTOK // 512):
            s = blk % 3
            self.dma(hin[s][:], hTv[:, :, blk * 512:(blk + 1) * 512], W=[hin[s]])
            for a in range(4):
                for half in range(2):
                    p = ps[pi % 6]
                    pi += 1
                    for j in range(4):
                        self.tp(p[:, j * 128:(j + 1) * 128], hin[s][:, half * 4 + j, a * 128:(a + 1) * 128], ident[:],
                                [hin[s], ident], [p])
                    self.cp('act' if half == 0 else 'dve', ht[s][:, a, half * 512:(half + 1) * 512], p[:], [p], [ht[s]])
                t_ = st[a]
                self.act(junk[:], ht[s][:, a, :], AF.Square, [ht[s]], [junk, t_], accum=t_[:, 0:1])
                self.act(t_[:, 1:2], t_[:, 0:1], AF.Sqrt, [t_], [t_], bias=EPS, scale=1.0 / D)
                self.rcp(t_[:, 2:3], t_[:, 1:2], [t_], [t_])
                self.stt(ht[s][:, a, :], ht[s][:, a, :], t_[:, 2:3], fgb[:], ALU.mult, ALU.mult, [ht[s], t_, fgb], [ht[s]])
            dst = self.y_s[blk * 512:(blk + 1) * 512, :] if blk < 8 else self.y_p
            self.dma(dst.rearrange("(a p) d -> p a d", p=128), ht[s][:], R=[ht[s]], owner=ht[s])
        self.end("out")

    def norm_block(self, hb, gi, modv, ones, sq, ps_ss, rtmp, rstd, tmps, xn):
        self.act(sq[:], hb[:], AF.Square, [hb], [sq])
        for k in range(8):
            self.mm(ps_ss[:], ones[:], sq[:, k, :], [ones, sq], [ps_ss], start=(k == 0), stop=(k == 7))
        self.act(rtmp[:], ps_ss[:], AF.Sqrt, [ps_ss], [rtmp], bias=EPS, scale=1.0 / D)
        self.rcp(rstd[:], rtmp[:], [rtmp], [rstd])
        for k in range(8):
            tm = tmps[k % 2]
            self.stt(tm[:], hb[:, k, :], modv[:, 16 + 2 * k + gi:16 + 2 * k + gi + 1], rstd[:], ALU.mult, ALU.mult,
                     [hb, modv, rstd], [tm])
            self.act(xn[:, k, :], tm[:], AF.Identity, [tm, modv], [xn], bias=modv[:, 2 * k + gi:2 * k + gi + 1], scale=1.0)

    def norm_block_g(self, hb, gi, modv, ones, sq, ps_ss, rtmp, rstd, tmps, xn):
        self.act(sq[:], hb[:], AF.Square, [hb], [sq])
        for k in range(8):
            self.mm(ps_ss[:], ones[:], sq[:, k, :], [ones, sq], [ps_ss], start=(k == 0), stop=(k == 7))
        self.act(rtmp[:], ps_ss[:], AF.Sqrt, [ps_ss], [rtmp], bias=EPS, scale=1.0 / D)
        self.rcp(rstd[:], rtmp[:], [rtmp], [rstd])
        yield
        for k in range(8):
            tm = tmps[k % 2]
            self.stt(tm[:], hb[:, k, :], modv[:, 16 + 2 * k + gi:16 + 2 * k + gi + 1], rstd[:], ALU.mult, ALU.mult,
                     [hb, modv, rstd], [tm])
            self.act(xn[:, k, :], tm[:], AF.Identity, [tm, modv], [xn], bias=modv[:, 2 * k + gi:2 * k + gi + 1], scale=1.0)
            if k % 2 == 1:
                yield


    def phase_p1(self, l):
        self.begin()
        ones = self.sb([128, 128], BF16); self.ms('pool', ones[:], 1.0, [ones])
        modv = self.sb([128, 48]); self.dma(modv[:], self.modv[l], W=[modv])
        adb = self.sb([128, 16])
        self.dma(adb[:, 0:8], self.a_log[l].partition_broadcast(128), W=[adb])
        self.dma(adb[:, 8:16], self.dt_bias[l].partition_broadcast(128), W=[adb])
        nea = self.sb([128, 8])
        self.act(nea[:], adb[:, 0:8], AF.Exp, [adb], [nea])
        self.ts('dve', nea[:], nea[:], -1.0, ALU.mult, [nea], [nea])
        w1 = self.sb([128, 8, 1552], BF16, "w1")
        w1B = [Buf(f"w1_{i}") for i in range(3)]
        w1src = self.w_in[l][:, 0:1552].rearrange("(k p) n -> p k n", p=128)
        for i, (c0_, c1_) in enumerate([(0, 512), (512, 1024), (1024, 1552)]):
            self.dma(w1[:, :, c0_:c1_], w1src[:, :, c0_:c1_], W=[w1B[i]], eng='pool')
        hb = [self.sb([128, 8, 512], F32, "hb") for _ in range(2)]
        xn = [self.sb([128, 8, 512], BF16, "xn") for _ in range(2)]
        sq = self.sb([128, 8, 512], BF16, "sq")
        rtmp = self.sb([128, 512]); rstd = self.sb([128, 512])
        tmps = [self.sb([128, 512], F32, "tm") for _ in range(2)]
        qst = [self.sb([128, 4, 512], F32, "qst") for _ in range(2)]
        gbt = [self.sb([128, 4, 16], F32, "gbt") for _ in range(2)]
        sm = [self.sb([128, 8], F32, "sm") for _ in range(4)]
        bdsb = self.sb([128, 64], F32, "bdsb")
        ps_ss = self.psum([128, 512])
        psq = [self.psum([128, 512]) for _ in range(4)]
        psb = self.psum([128, 512])
        hTv = self.hT.rearrange("(k p) t -> p k t", p=128)
        xnTv = self.xnT.rearrange("(k p) t -> p k t", p=128)
        qv = self.qkvT.rearrange("(j p) t -> p j t", p=128)
        cnt = [0]

        def sa(blk):
            s = blk % 2
            gi = 0 if blk < 8 else 1
            c0 = blk * 512
            self.dma(hb[s][:], hTv[:, :, c0:c0 + 512], W=[hb[s]])
            for _ in self.norm_block_g(hb[s], gi, modv, ones, sq, ps_ss, rtmp, rstd, tmps, xn[s]):
                yield
            self.dma(xnTv[:, :, c0:c0 + 512], xn[s][:], R=[xn[s]], owner=xn[s])

        def sb_(blk):
            s = blk % 2
            c0 = blk * 512
            for j in range(12):
                p = psq[cnt[0] % 4]
                cnt[0] += 1
                for k in range(8):
                    self.mm(p[:], w1[:, k, j * 128:(j + 1) * 128], xn[s][:, k, :], [w1B[j // 4], xn[s]], [p],
                            start=(k == 0), stop=(k == 7))
                q = qst[(j // 4) % 2]
                self.cp('act' if j % 2 == 0 else 'dve', q[:, j % 4, :], p[:], [p], [q])
                if j % 4 == 3:
                    self.dma(qv[:, j - 3:j + 1, c0:c0 + 512], q[:], R=[q], owner=q)
                yield
            g_ = gbt[s]
            for tl in range(4):
                for k in range(8):
                    self.mm(psb[:, tl * 16:(tl + 1) * 16], xn[s][:, k, tl * 128:(tl + 1) * 128], w1[:, k, 1536:1552],
                            [xn[s], w1B[2]], [psb], start=(k == 0), stop=(k == 7))
            self.cp('act', bdsb[:], psb[:, 0:64], [psb], [bdsb])
            yield
            for tl in range(4):
                pb_ = bdsb[:, tl * 16:tl * 16 + 8]
                pd_ = bdsb[:, tl * 16 + 8:tl * 16 + 16]
                self.act(sm[0][:], pb_, AF.Exp, [bdsb], [sm[0]], scale=-1.0)
                self.ts('dve', sm[0][:], sm[0][:], 1.0, ALU.add, [sm[0]], [sm[0]])
                self.rcp(g_[:, tl, 0:8], sm[0][:], [sm[0]], [g_])
                self.tt('dve', sm[1][:], pd_, adb[:, 8:16], ALU.add, [bdsb, adb], [sm[1]])
                self.stt(sm[2][:], sm[1][:], -1.0, sm[1][:], ALU.mult, ALU.max, [sm[1]], [sm[2]])
                self.act(sm[2][:], sm[2][:], AF.Exp, [sm[2]], [sm[2]], scale=-1.0)
                self.act(sm[2][:], sm[2][:], AF.Ln, [sm[2]], [sm[2]], bias=1.0, scale=1.0)
                self.stt(sm[3][:], sm[1][:], 0.0, sm[2][:], ALU.max, ALU.add, [sm[1], sm[2]], [sm[3]])
                self.tt('dve', g_[:, tl, 8:16], sm[3][:], nea[:], ALU.mult, [sm[3], nea], [g_])
                yield
            self.dma(self.gb[c0:c0 + 512, :].rearrange("(a p) c -> p a c", p=128), g_[:], R=[g_], owner=g_)

        nblk = NTOK // 512
        for i in range(nblk + 1):
            gs = []
            if i < nblk:
                gs.append(sa(i))
            if i >= 1:
                gs.append(sb_(i - 1))
            while gs:
                for g in list(gs):
                    try:
                        next(g)
                    except StopIteration:
                        gs.remove(g)
        self.end(f"p1_{l}")

    def phase_four(self, l):
        self.begin()
        stg = None
        wxc = self.sb([128, 8, 512], BF16, "wxc")
        self.load_cast(wxc, 0, self.w_in[l][:, OFF['xc']:OFF['xc'] + 512], 8, 512, stg)
        cf = self.sb([128, 512 + 256 + 1024]);
        self.dma(cf[:, 0:512], self.c_cs4, W=[cf])
        self.dma(cf[:, 512:768], self.c_bd.rearrange("p a b -> p (a b)"), W=[cf])
        self.dma(cf[:, 768:1792], self.c_cp.rearrange("p a b c -> p (a b c)"), W=[cf])
        cb = self.sb([128, 1792], BF16)
        self.cp('dve', cb[:], cf[:], [cf], [cb])
        cs4 = cb[:, 0:512]
        bdc = cb[:, 512:640]; bds = cb[:, 640:768]
        cpm = cb[:, 768:1792].rearrange("p (a b c) -> p a b c", a=2, b=2)
        xn = [self.sb([128, 8, 512], BF16, "xn") for _ in range(2)]
        xc2 = [self.sb([128, 4, 512], BF16, "xc") for _ in range(2)]
        ab = self.sb([128, 4, 4, 512], BF16, "ab")
        pq = [self.sb([128, 4, 4, 256], BF16, "pq") for _ in range(2)]
        ysb = [self.sb([128, 256], F32, "ysb") for _ in range(2)]
        psA = [self.psum([128, 512]) for _ in range(2)]
        ps = [self.psum([128, 512]) for _ in range(5)]
        xnTv = self.xnT.rearrange("(k p) t -> p k t", p=128)
        cnt = [0, 0, 0]

        def sa(blk):
            s = blk % 2
            c0 = blk * 512
            xc = xc2[s]
            self.dma(xn[s][:], xnTv[:, :, c0:c0 + 512], W=[xn[s]])
            for g in range(4):
                p = psA[cnt[0] % 2]; cnt[0] += 1
                for k in range(8):
                    self.mm(p[:], wxc[:, k, g * 128:(g + 1) * 128], xn[s][:, k, :], [wxc, xn[s]], [p],
                            start=(k == 0), stop=(k == 7))
                self.cp('act' if g % 2 == 0 else 'dve', xc[:, g, :], p[:], [p], [xc])
                yield

        def sb_(blk):
            s = blk % 2
            c0 = blk * 512
            xc = xc2[s]
            for tl in range(4):
                for g in range(4):
                    p = ps[cnt[1] % 5]; cnt[1] += 1
                    self.mm(p[:], xc[:, g, tl * 128:(tl + 1) * 128], cs4, [xc, cb], [p])
                    self.cp('act' if cnt[2] % 2 == 0 else 'dve', ab[:, tl, g, :], p[:], [p], [ab]); cnt[2] += 1
                yield
            if blk < 8:
                for tl in range(4):
                    for g2 in range(2):
                        p = ps[cnt[1] % 5]; cnt[1] += 1
                        for gg in range(2):
                            g = g2 * 2 + gg
                            self.mm(p[:, gg * 256:(gg + 1) * 256], bdc, ab[:, tl, g, 0:256], [cb, ab], [p], start=True, stop=False)
                            self.mm(p[:, gg * 256:(gg + 1) * 256], bds, ab[:, tl, g, 256:512], [cb, ab], [p], start=False, stop=True)
                        self.cp('act' if cnt[2] % 2 == 0 else 'dve', pq[s][:, tl, g2 * 2:g2 * 2 + 2, :],
                                p[:].rearrange("p (a b) -> p a b", a=2), [p], [pq[s]]); cnt[2] += 1
                    yield
                self.dma(self.PQ[c0:c0 + 512].rearrange("(a p) g c -> p a g c", p=128), pq[s][:], R=[pq[s]], owner=pq[s])
            else:
                for sq_ in range(2):
                    for g in range(4):
                        p = ps[cnt[1] % 5]; cnt[1] += 1
                        n = 0
                        for tc in range(2):
                            for m_ in range(2):
                                self.mm(p[:, 0:256], ab[:, sq_ * 2 + tc, g, m_ * 128:(m_ + 1) * 128], cpm[:, m_, tc, :],
                                        [ab, cb], [p], start=(n == 0), stop=(n == 3))
                                n += 1
                        y = ysb[cnt[2] % 2]
                        self.cp('act' if cnt[2] % 2 == 0 else 'dve', y[:], p[:, 0:256], [p], [y]); cnt[2] += 1
                        self.dma(self.yT[g * 128:(g + 1) * 128, NS + sq_ * 256:NS + (sq_ + 1) * 256], y[:], R=[y], owner=y)
                    yield

        nblk = NTOK // 512
        for i in range(nblk + 1):
            gs = []
            if i < nblk:
                gs.append(sa(i))
            if i >= 1:
                gs.append(sb_(i - 1))
            while gs:
                for g in list(gs):
                    try:
                        next(g)
                    except StopIteration:
                        gs.remove(g)
        self.end(f"four_{l}")

    def phase_four3(self, l):
        self.begin()
        cf = self.sb([128, 64]); self.dma(cf[:], self.c_csr, W=[cf])
        csr = self.sb([128, 64], BF16); self.cp('dve', csr[:], cf[:], [cf], [csr])
        pq2 = [self.sb([128, 64, 128], BF16, "pq2") for _ in range(2)]
        ysb = [self.sb([128, 64, 64], F32, "ysb") for _ in range(2)]
        ps = [self.psum([128, 8, 64]) for _ in range(4)]
        pqv = self.PQ.rearrange("(r c) g (q h) -> q r g c h", c=64, q=2)
        pi = 0
        for g in range(4):
            s = g % 2
            for q in range(2):
                self.dma(pq2[s][q * 64:(q + 1) * 64, :, :], pqv[q, :, g, :, :], W=[pq2[s]])
            for c8 in range(8):
                p = ps[pi % 4]; pi += 1
                for cc in range(8):
                    c = c8 * 8 + cc
                    self.mm(p[:, cc, :], pq2[s][:, c, :], csr[:], [pq2[s], csr], [p])
                self.cp('act' if c8 % 2 == 0 else 'dve', ysb[s][:, :, c8 * 8:(c8 + 1) * 8],
                        p[:].rearrange("p c r -> p r c"), [p], [ysb[s]])
            self.dma(self.yT[g * 128:(g + 1) * 128, 0:NS], ysb[s][:].rearrange("p r c -> p (r c)"), R=[ysb[s]], owner=ysb[s])
        self.end(f"four3_{l}")

    def phase_delta(self, l):
        self.begin()
        ident = self.sb([128, 128]); self.dma(ident[:], self.c_ident, W=[ident])
        identb = self.sb([128, 128], BF16); self.cp('dve', identb[:], ident[:], [ident], [identb])
        tri = self.sb([128, 6, 128]); self.dma(tri[:], self.c_tri, W=[tri])
        ones = self.sb([128, 128], BF16); self.ms('pool', ones[:], 1.0, [ones])
        cw = self.sb([128, 36])
        pS = [self.psum([128, 512]) for _ in range(2)]
        self.load_T(cw[:], cw, self.conv_w[l], 36, ident, pS[0])
        cst = dict(ident=ident, identb=identb, tri=tri, ones=ones, cw=cw)
        gens = [self._delta_stream(l, d, cst, pS[d]) for d in range(2)]
        active = list(gens)
        while active:
            for g in list(active):
                try:
                    next(g)
                except StopIteration:
                    active.remove(g)
        self.end(f"delta_{l}")

    def _delta_stream(self, l, d, cst, pS):
        ident, identb, tri, ones, cw = cst['ident'], cst['identb'], cst['tri'], cst['ones'], cst['cw']
        U = tri[:, 0 + d, :]
        mA = tri[:, 2 + d, :]
        mQ = tri[:, 0 + d, :]
        last = 127 if d == 0 else 0
        cwv = cw[:].rearrange("p (k j) -> p k j", k=3)
        two = lambda shape, dt=F32: [self.sb(shape, dt) for _ in range(2)]
        xin = two([128, 12, 130]); gbi = two([128, 16])
        c0t = self.sb([128, 12, 128]); c1t = self.sb([128, 12, 128])
        ybf2 = two([128, 8, 128], BF16); yv = self.sb([128, 4, 128])
        sq = self.sb([128, 8, 128], BF16)
        sc2 = two([128, 64])
        gbc = self.sb([128, 4, 128]); d1 = self.sb([128, 4, 128])
        Wt2 = two([128, 4, 128]); eG2 = two([128, 4, 128])
        WL = self.sb([128, 4, 128]); WU = self.sb([128, 4, 128])
        Ab = two([128, 4, 128]); Bb = two([128, 4, 128]); Pb = two([128, 4, 128])
        Af = self.sb([128, 4, 128]); Aoff = self.sb([128, 4, 128]); Td = self.sb([128, 4, 128]); M1 = self.sb([128, 4, 128])
        Pfin2 = two([128, 4, 128], BF16)
        mBD = tri[:, 4, :].unsqueeze(1).to_broadcast([128, 4, 128])
        mOFF = tri[:, 5, :].unsqueeze(1).to_broadcast([128, 4, 128])
        qkT2 = two([128, 4, 128], BF16); qgT2 = two([128, 4, 128], BF16)
        ktok2 = two([128, 4, 128], BF16); vbr2 = two([128, 4, 128])
        Yr = self.sb([128, 4, 128], BF16); usb = self.sb([128, 4, 128], BF16); ud = self.sb([128, 4, 128], BF16)
        osb = two([128, 4, 128])
        S = self.sb([128, 4, 128]); Sb = self.sb([128, 4, 128], BF16)
        pX = self.psum([128, 4, 128]); pY = self.psum([128, 4, 128]); pZ = self.psum([128, 4, 128])
        psm = pS[:, 256:512]
        pT = pS[:, 0:256].bitcast(BF16).rearrange("p (h v) -> p h v", h=4)
        SSQ, RQ, RK2, RK, GAM, EG, BRK, BRK2, C1N, T0 = [slice(i * 4, i * 4 + 4) for i in range(10)]
        I4 = ident[:].unsqueeze(1).to_broadcast([128, 4, 128])

        def bc(ap):
            return ap.unsqueeze(2).to_broadcast([128, 4, 128])

        qv = self.qkvT.rearrange("(j p) t -> p j t", p=128)

        def prep(ch, p):
            si, c, nch, t0 = ch
            x = xin[p]; gbx = gbi[p]; ybf = ybf2[p]; sc = sc2[p]; Wt = Wt2[p]; eG = eG2[p]
            qkT = qkT2[p]; qgT = qgT2[p]; ktok = ktok2[p]; vbr = vbr2[p]; Pfin = Pfin2[p]
            lo = 1 if c == 0 else 0
            hi = 129 if c == nch - 1 else 130
            if c == 0:
                self.ms('pool', x[:, :, 0:1], 0.0, [x])
            if c == nch - 1:
                self.ms('pool', x[:, :, 129:130], 0.0, [x])
            for jj in range(3):
                self.dma(x[:, jj * 4:(jj + 1) * 4, lo:hi], qv[:, jj * 4:(jj + 1) * 4, t0 - 1 + lo:t0 - 1 + hi], W=[x])
            self.dma(gbx[:], self.gb[t0:t0 + 128, :], W=[gbx])
            for k in range(3):
                wb = cwv[:, k, :].unsqueeze(2).to_broadcast([128, 12, 128])
                dst = c0t if k == 0 else c1t
                self.tt('pool', dst[:], x[:, :, k:k + 128], wb, ALU.mult, [x, cw], [dst])
                if k > 0:
                    self.tt('pool', c0t[:], c0t[:], c1t[:], ALU.add, [c0t, c1t], [c0t])
            self.act(ybf[:], c0t[:, 0:8, :], AF.Silu, [c0t], [ybf])
            self.act(yv[:], c0t[:, 8:12, :], AF.Silu, [c0t], [yv])
            yield
            self.tt('pool', sq[:], ybf[:], ybf[:], ALU.mult, [ybf], [sq])
            for j in range(8):
                self.mm(psm[:, j:j + 1], sq[:, j, :], ones[:, 0:1], [sq, ones], [pS])
            self.cp('act', sc[:, 48:56], psm[:, 0:8], [pS], [sc])
            self.act(sc[:, SSQ], sc[:, 48:52], AF.Sqrt, [sc], [sc], bias=128.0 * EPS, scale=128.0)
            self.rcp(sc[:, RQ], sc[:, SSQ], [sc], [sc])
            self.ts('dve', sc[:, T0], sc[:, 52:56], EPS, ALU.add, [sc], [sc])
            self.rcp(sc[:, RK2], sc[:, T0], [sc], [sc])
            self.act(sc[:, RK], sc[:, RK2], AF.Sqrt, [sc], [sc])
            b_ = gbx[:, d * 4:(d + 1) * 4]
            g_ = gbx[:, 8 + d * 4:8 + (d + 1) * 4]
            self.mm(psm[:, 16:20], U, g_, [tri, gbx], [pS])
            self.cp('dve', sc[:, GAM], psm[:, 16:20], [pS], [sc])
            self.act(sc[:, EG], sc[:, GAM], AF.Exp, [sc], [sc])
            self.tt('dve', sc[:, BRK], b_, sc[:, RK], ALU.mult, [gbx, sc], [sc])
            self.tt('dve', sc[:, BRK2], b_, sc[:, RK2], ALU.mult, [gbx, sc], [sc])
            self.stt(sc[:, C1N], sc[:, BRK2], -1.0, sc[:, EG], ALU.mult, ALU.mult, [sc], [sc])
            yield
            self.cp('dve', gbc[:], bc(g_), [gbx], [gbc])
            for h in range(4):
                self.mm(pX[:, h, :], gbc[:, h, :], U, [gbc, tri], [pX])
            self.tt('dve', d1[:], pX[:], bc(sc[:, GAM]), ALU.subtract, [pX, sc], [d1])
            self.stt(d1[:], d1[:], -1.0, d1[:], ALU.mult, ALU.min, [d1], [d1])
            self.act(Wt[:], d1[:], AF.Exp, [d1], [Wt])
            self.act(eG[:], pX[:], AF.Exp, [pX], [eG])
            self.tt('pool', WL[:], Wt[:], mA.unsqueeze(1).to_broadcast([128, 4, 128]), ALU.mult, [Wt, tri], [WL])
            self.tt('pool', WU[:], Wt[:], mQ.unsqueeze(1).to_broadcast([128, 4, 128]), ALU.mult, [Wt, tri], [WU])
            yield
            for h in range(4):
                self.mm(pX[:, h, :], ybf[:, 4 + h, :], ybf[:, 4 + h, :], [ybf], [pX])
                self.mm(pY[:, h, :], ybf[:, 4 + h, :], ybf[:, h, :], [ybf], [pY])
            A0 = Ab[0]
            for h in range(4):
                self.stt(Af[:, h, :], pX[:, h, :], sc[:, BRK2.start + h:BRK2.start + h + 1], WL[:, h, :],
                         ALU.mult, ALU.mult, [pX, sc, WL], [Af])
            self.tt('pool', A0[:], Af[:], mBD, ALU.mult, [Af, tri], [A0])
            self.tt('pool', Aoff[:], Af[:], mOFF, ALU.mult, [Af, tri], [Aoff])
            self.tt('dve', qkT[:], pY[:], WU[:], ALU.mult, [pY, WU], [qkT])
            self.tt('dve', qgT[:], ybf[:, 0:4, :], eG[:], ALU.mult, [ybf, eG], [qgT])
            yield
            for h in range(4):
                self.tp(pT[:, h, :], ybf[:, 4 + h, :], identb[:], [ybf, identb], [pS])
            self.cp('act', ktok[:], pT, [pS], [ktok])
            for h in range(4):
                self.tp(pX[:, h, :], yv[:, h, :], ident[:], [yv, ident], [pX])
            self.tt('dve', vbr[:], pX[:], bc(sc[:, BRK]), ALU.mult, [pX, sc], [vbr])
            yield
            for h in range(4):
                self.tp(pY[:, h, :], A0[:, h, :], ident[:], [A0, ident], [pY])
            B0 = Bb[0]
            self.cp('act', B0[:], pY[:], [pY], [B0])
            P0 = Pb[0]
            self.stt(P0[:], B0[:], -1.0, I4, ALU.mult, ALU.add, [B0, ident], [P0])
            yield
            cA, cB, cP = 0, 0, 0
            NL_ = 5
            for lvl in range(NL_):
                nA = Ab[1 - cA]; nB = Bb[1 - cB]
                for h in range(4):
                    self.mm(pX[:, h, :], Bb[cB][:, h, :], Ab[cA][:, h, :], [Bb[cB], Ab[cA]], [pX])
                if lvl < NL_ - 1:
                    for h in range(4):
                        self.mm(pY[:, h, :], Ab[cA][:, h, :], Bb[cB][:, h, :], [Bb[cB], Ab[cA]], [pY])
                self.cp('act', nA[:], pX[:], [pX], [nA])
                if lvl < NL_ - 1:
                    self.cp('act', nB[:], pY[:], [pY], [nB])
                yield
                for h in range(4):
                    self.mm(pX[:, h, :], nA[:, h, :], Pb[cP][:, h, :], [nA, Pb[cP]], [pX])
                self.tt('dve', Pb[1 - cP][:], pX[:], Pb[cP][:], ALU.add, [pX, Pb[cP]], [Pb[1 - cP]])
                cA, cB, cP = 1 - cA, 1 - cB, 1 - cP
                yield
            Pd = Pb[cP]
            for h in range(4):
                self.tp(pX[:, h, :], Pd[:, h, :], ident[:], [Pd, ident], [pX])
            self.cp('act', Td[:], pX[:], [pX], [Td])
            for h in range(4):
                self.mm(pY[:, h, :], Aoff[:, h, :], Pd[:, h, :], [Aoff, Pd], [pY])
            self.cp('act', M1[:], pY[:], [pY], [M1])
            yield
            for h in range(4):
                self.mm(pX[:, h, :], Td[:, h, :], M1[:, h, :], [Td, M1], [pX])
            self.tt('dve', Pfin[:], Pd[:], pX[:], ALU.subtract, [Pd, pX], [Pfin])
            yield

        def scan(ch, p):
            si, c, nch, t0 = ch
            ybf = ybf2[p]; sc = sc2[p]; Wt = Wt2[p]; eG = eG2[p]
            qkT = qkT2[p]; qgT = qgT2[p]; ktok = ktok2[p]; vbr = vbr2[p]; P = Pfin2[p]
            first = (c == 0) if d == 0 else (c == nch - 1)
            lastc = (c == nch - 1) if d == 0 else (c == 0)
            if first:
                if si == 0:
                    self.dma(S[:], self.state[l, d].rearrange("h k v -> k h v"), W=[S])
                else:
                    self.ms('pool', S[:], 0.0, [S])
                self.cp('act', Sb[:], S[:], [S], [Sb])
            for h in range(4):
                self.mm(pZ[:, h, :], ybf[:, 4 + h, :], Sb[:, h, :], [ybf, Sb], [pZ])
            for h in range(4):
                self.stt(Yr[:, h, :], pZ[:, h, :], sc[:, C1N.start + h:C1N.start + h + 1], vbr[:, h, :],
                         ALU.mult, ALU.add, [pZ, sc, vbr], [Yr])
            yield
            for h in range(4):
                self.mm(pZ[:, h, :], P[:, h, :], Yr[:, h, :], [P, Yr], [pZ])
            self.cp('dve', usb[:], pZ[:], [pZ], [usb])
            for h in range(4):
                self.ts('dve', ud[:, h, :], pZ[:, h, :], Wt[:, h, last:last + 1], ALU.mult, [pZ, Wt], [ud])
            yield
            for h in range(4):
                self.mm(pZ[:, h, :], qgT[:, h, :], Sb[:, h, :], [qgT, Sb], [pZ], start=True, stop=False)
                self.mm(pZ[:, h, :], qkT[:, h, :], usb[:, h, :], [qkT, usb], [pZ], start=False, stop=True)
            o_ = osb[p]
            self.tt('dve', o_[:], pZ[:], bc(sc[:, RQ]), ALU.mult, [pZ, sc], [o_])
            self.dma(self.oFB[d][t0:t0 + 128, :], o_[:].rearrange("p h v -> p (h v)"), R=[o_], owner=o_)
            yield
            for h in range(4):
                self.mm(pZ[:, h, :], ktok[:, h, :], ud[:, h, :], [ktok, ud], [pZ])
            for h in range(4):
                self.stt(S[:, h, :], S[:, h, :], eG[:, h, last:last + 1], pZ[:, h, :], ALU.mult, ALU.add,
                         [S, eG, pZ], [S])
            self.cp('act', Sb[:], S[:], [S], [Sb])
            if lastc and si > 0:
                self.dma(self.ns[si - 1, l, d].rearrange("h k v -> k h v"), S[:], R=[S], owner=S)
            yield

        chunks = []
        for si, (seg0, segT) in enumerate(SEGS):
            nch = segT // 128
            order = range(nch) if d == 0 else range(nch - 1, -1, -1)
            for c in order:
                chunks.append((si, c, nch, seg0 + c * 128))
        for i in range(len(chunks) + 1):
            gs = []
            if i < len(chunks):
                gs.append(prep(chunks[i], i % 2))
            if i >= 1:
                gs.append(scan(chunks[i - 1], (i - 1) % 2))
            while gs:
                for g in list(gs):
                    try:
                        next(g)
                    except StopIteration:
                        gs.remove(g)
                        continue
                    yield

    def phase_p5a(self, l):
        self.begin()
        ident = self.sb([128, 128]); self.dma(ident[:], self.c_ident, W=[ident])
        identb = self.sb([128, 128], BF16); self.cp('dve', identb[:], ident[:], [ident], [identb])
        stg = None
        w5 = self.sb([128, 8, 2560], BF16, "w5")
        self.load_cast(w5, 0, self.w_in[l][:, OFF['za']:OFF['za'] + 2048], 8, 2048, stg)
        self.load_cast(w5, 2048, self.w_in[l][:, OFF['zc']:OFF['zc'] + 512], 8, 512, stg)
        ZA, UB, VB, ZB, ZC = 0, 512, 1024, 1536, 2048
        ongb = self.sb([128, 4, 128])
        for h in range(4):
            self.dma(ongb[:, h, :], self.o_norm_g[l].partition_broadcast(128), W=[ongb])
        sgb = self.sb([128, 512]); self.dma(sgb[:], self.sgu_g[l].partition_broadcast(128), W=[sgb])
        bsb = self.sb([128, 4, 128]); self.dma(bsb[:].rearrange("p g q -> p (g q)"), self.b_sp[l].partition_broadcast(128), W=[bsb])
        wsf = self.sb([128, 4, 128]); self.dma(wsf[:], self.w_sp[l].rearrange("g p q -> p g q"), W=[wsf])
        wsT = self.sb([128, 4, 128], BF16)
        psx = self.psum([128, 4, 128])
        for g in range(4):
            self.tp(psx[:, g, :], wsf[:, g, :], ident[:], [wsf, ident], [psx])
        self.cp('dve', wsT[:], psx[:], [psx], [wsT])

        xn = [self.sb([128, 8, 512], BF16, "xn") for _ in range(2)]
        oF = [self.sb([128, 4, 512], F32, "oF") for _ in range(2)]
        oB = [self.sb([128, 4, 512], F32, "oB") for _ in range(2)]
        yt = [self.sb([128, 4, 512], F32, "yt") for _ in range(2)]
        br = [self.sb([128, 12, 512], BF16, "br") for _ in range(2)]
        ubT = self.sb([128, 4, 512]); szb = self.sb([128, 4, 512])
        uz2 = [self.sb([128, 4, 512], F32, "uz") for _ in range(2)]
        szc = [self.sb([128, 512], F32, "szc") for _ in range(2)]
        zas = self.sb([128, 512]); osum = self.sb([128, 512]); t1 = self.sb([128, 512])
        junkA = self.sb([128, 128]); junkB = self.sb([128, 512])
        oab = self.sb([128, 512], BF16)
        vbs = self.sb([128, 512]); vn = self.sb([128, 512], BF16); vs = self.sb([128, 4, 128])
        stA = self.sb([128, 16]); stB = self.sb([128, 16])
        ps = [self.psum([128, 512]) for _ in range(5)]
        pT = self.psum([128, 8, 128], BF16)
        xnTv = self.xnT.rearrange("(k p) t -> p k t", p=128)
        brv = self.brT.rearrange("(j p) t -> p j t", p=128)
        yTv = self.yT.rearrange("(g p) t -> p g t", p=128)
        cnt = [0]

        def proj_fm(p, col0, x):
            for k in range(8):
                self.mm(p[:], w5[:, k, col0:col0 + 128], x[:, k, :], [w5, x], [p], start=(k == 0), stop=(k == 7))

        def proj_tm(p, col0, x, tl):
            for k in range(8):
                self.mm(p[:], x[:, k, tl * 128:(tl + 1) * 128], w5[:, k, col0:col0 + 512], [w5, x], [p],
                        start=(k == 0), stop=(k == 7))

        def fm(blk):
            s = blk % 2
            c0 = blk * 512
            x = xn[s]; b_ = br[s]; uz = uz2[s]
            self.dma(x[:], xnTv[:, :, c0:c0 + 512], W=[x])
            self.dma(oF[s][:], self.oFB[0][c0:c0 + 512, :].rearrange("(a p) c -> p a c", p=128), W=[oF[s]])
            self.dma(oB[s][:], self.oFB[1][c0:c0 + 512, :].rearrange("(a p) c -> p a c", p=128), W=[oB[s]])
            self.dma(yt[s][:], yTv[:, :, c0:c0 + 512], W=[yt[s]])
            for g in range(4):
                p = ps[cnt[0] % 3]; cnt[0] += 1
                proj_fm(p, UB + g * 128, x)
                self.cp('dve', ubT[:, g, :], p[:], [p], [ubT])
                yield
                p = ps[cnt[0] % 3]; cnt[0] += 1
                proj_fm(p, ZB + g * 128, x)
                self.act(szb[:, g, :], p[:], AF.Silu, [p], [szb])
                yield
            self.tt('pool', uz[:], ubT[:], szb[:], ALU.mult, [ubT, szb], [uz])
            for g in range(4):
                p = ps[cnt[0] % 3]; cnt[0] += 1
                proj_fm(p, ZC + g * 128, x)
                z = szc[g % 2]
                self.act(z[:], p[:], AF.Silu, [p], [z])
                self.tt('dve', b_[:, 8 + g, :], z[:], yt[s][:, g, :], ALU.mult, [z, yt[s]], [b_])
                yield

        def ta(blk):
            s = blk % 2
            x = xn[s]; b_ = br[s]
            p = ps[3]
            for tl in range(4):
                tsl = slice(tl * 128, (tl + 1) * 128)
                proj_tm(p, ZA, x, tl)
                self.act(zas[:], p[:], AF.Silu, [p], [zas])
                self.tt('dve', osum[:], oF[s][:, tl, :], oB[s][:, tl, :], ALU.add, [oF[s], oB[s]], [osum])
                yield
                for h in range(4):
                    self.act(junkA[:], osum[:, h * 128:(h + 1) * 128], AF.Square, [osum], [junkA, stA],
                             accum=stA[:, h:h + 1])
                self.act(stA[:, 4:8], stA[:, 0:4], AF.Sqrt, [stA], [stA], bias=EPS, scale=1.0 / 128)
                self.rcp(stA[:, 8:12], stA[:, 4:8], [stA], [stA])
                yield
                self.tt('dve', t1[:].rearrange("p (h v) -> p h v", h=4), osum[:].rearrange("p (h v) -> p h v", h=4),
                        stA[:, 8:12].unsqueeze(2).to_broadcast([128, 4, 128]), ALU.mult, [osum, stA], [t1])
                self.tt('pool', t1[:], t1[:], ongb[:].rearrange("p h v -> p (h v)"), ALU.mult, [t1, ongb], [t1])
                self.tt('dve', oab[:], t1[:], zas[:], ALU.mult, [t1, zas], [oab])
                yield
                for h in range(4):
                    self.tp(pT[:, h, :], oab[:, h * 128:(h + 1) * 128], identb[:], [oab, identb], [pT])
                self.cp('act', b_[:, 0:4, tsl], pT[:, 0:4, :], [pT], [b_])
                yield

        def tb(blk):
            s = blk % 2
            x = xn[s]; b_ = br[s]; uz = uz2[s]
            p = ps[4]
            for tl in range(4):
                tsl = slice(tl * 128, (tl + 1) * 128)
                proj_tm(p, VB, x, tl)
                self.cp('act', vbs[:], p[:], [p], [vbs])
                yield
                self.act(junkB[:], vbs[:], AF.Square, [vbs], [junkB, stB], accum=stB[:, 12:13])
                self.act(stB[:, 13:14], stB[:, 12:13], AF.Sqrt, [stB], [stB], bias=EPS, scale=1.0 / 512)
                self.rcp(stB[:, 14:15], stB[:, 13:14], [stB], [stB])
                self.stt(vn[:], vbs[:], stB[:, 14:15], sgb[:], ALU.mult, ALU.mult, [vbs, stB, sgb], [vn])
                yield
                for g in range(4):
                    self.mm(psx[:, g, :], vn[:, g * 128:(g + 1) * 128], wsT[:, g, :], [vn, wsT], [psx])
                self.tt('dve', vs[:], psx[:], bsb[:], ALU.add, [psx, bsb], [vs])
                self.tt('dve', b_[:, 4:8, tsl], vs[:], uz[:, :, tsl], ALU.mult, [vs, uz], [b_])
                yield

        nblk = NTOK // 512
        for i in range(nblk + 1):
            gs = []
            if i < nblk:
                gs.append(fm(i))
            if i >= 1:
                gs += [ta(i - 1), tb(i - 1)]
            while gs:
                for g in list(gs):
                    try:
                        next(g)
                    except StopIteration:
                        gs.remove(g)
            if i >= 1:
                b_ = br[(i - 1) % 2]
                self.dma(brv[:, :, (i - 1) * 512:i * 512], b_[:], R=[b_], owner=b_)
        self.end(f"p5a_{l}")

    def phase_p5b(self, l):
        self.begin()
        modv = self.sb([128, 48]); self.dma(modv[:], self.modv[l], W=[modv])
        stg = None
        stg12 = None
        wg = self.sb([128, 8, 3072], BF16, "wg")
        wbr = self.sb([128, 12, 1024], BF16, "wbr")
        wo = self.sb([128, 8, 1024], BF16, "wo")
        wgB = [Buf(f"wg{i}") for i in range(6)]
        wbrB = [Buf(f"wbr{i}") for i in range(2)]
        woB = [Buf(f"wo{i}") for i in range(2)]
        gsrc = self.w_in[l][:, OFF['gates']:OFF['gates'] + 3072].rearrange("(k p) n -> p k n", p=128)
        bsrc = self.w_br[l].rearrange("(k p) n -> p k n", p=128)
        osrc = self.w_out[l].rearrange("(k p) n -> p k n", p=128)
        for h in range(2):
            for n in range(3):
                c = n * 1024 + h * 512
                self.dma(wg[:, :, c:c + 512], gsrc[:, :, c:c + 512], W=[wgB[n * 2 + h]], eng='pool')
            self.dma(wbr[:, :, h * 512:(h + 1) * 512], bsrc[:, :, h * 512:(h + 1) * 512], W=[wbrB[h]], eng='pool')
        for h in range(2):
            self.dma(wo[:, :, h * 512:(h + 1) * 512], osrc[:, :, h * 512:(h + 1) * 512], W=[woB[h]], eng='pool')
        xn = [self.sb([128, 8, 512], BF16, "xn") for _ in range(2)]
        br = [self.sb([128, 12, 512], BF16, "br") for _ in range(2)]
        hb = [self.sb([128, 8, 512], F32, "hb") for _ in range(2)]
        mg = self.sb([128, 8, 512], BF16, "mg")
        sig = [self.sb([128, 512], F32, "sig") for _ in range(2)]
        macc = self.sb([128, 512]); mt = self.sb([128, 512])
        psg = [self.psum([128, 512]) for _ in range(3)]
        psp = [self.psum([128, 512]) for _ in range(3)]
        pso = [self.psum([128, 512]) for _ in range(2)]
        xnTv = self.xnT.rearrange("(k p) t -> p k t", p=128)
        brv = self.brT.rearrange("(j p) t -> p j t", p=128)
        hTv = self.hT.rearrange("(k p) t -> p k t", p=128)
        gi_, pi_, si_ = 0, 0, 0
        for blk in range(NTOK // 512):
            s = blk % 2
            gi = 0 if blk < 8 else 1
            c0 = blk * 512
            x = xn[s]; b_ = br[s]; h_ = hb[s]
            self.dma(x[:], xnTv[:, :, c0:c0 + 512], W=[x])
            self.dma(b_[:], brv[:, :, c0:c0 + 512], W=[b_])
            self.dma(h_[:], hTv[:, :, c0:c0 + 512], W=[h_])
            for j in range(8):
                for n in range(3):
                    pg = psg[gi_ % 3]; gi_ += 1
                    for k in range(8):
                        self.mm(pg[:], wg[:, k, n * 1024 + j * 128:n * 1024 + (j + 1) * 128], x[:, k, :], [wgB[n * 2 + j // 4], x], [pg],
                                start=(k == 0), stop=(k == 7))
                    sg = sig[si_ % 2]; si_ += 1
                    self.act(sg[:], pg[:], AF.Sigmoid, [pg], [sg])
                    pp = psp[pi_ % 3]; pi_ += 1
                    for wc in range(4):
                        self.mm(pp[:], wbr[:, n * 4 + wc, j * 128:(j + 1) * 128], b_[:, n * 4 + wc, :], [wbrB[j // 4], b_], [pp],
                                start=(wc == 0), stop=(wc == 3))
                    if n == 0:
                        self.tt('dve', macc[:], sg[:], pp[:], ALU.mult, [sg, pp], [macc])
                    elif n == 1:
                        self.tt('dve', mt[:], sg[:], pp[:], ALU.mult, [sg, pp], [mt])
                        self.tt('pool', macc[:], macc[:], mt[:], ALU.add, [macc, mt], [macc])
                    else:
                        self.tt('dve', mt[:], sg[:], pp[:], ALU.mult, [sg, pp], [mt])
                        self.tt('dve', mg[:, j, :], macc[:], mt[:], ALU.add, [macc, mt], [mg])
            for e in range(8):
                po = pso[e % 2]
                for j in range(8):
                    self.mm(po[:], wo[:, j, e * 128:(e + 1) * 128], mg[:, j, :], [woB[e // 4], mg], [po], start=(j == 0), stop=(j == 7))
                self.stt(h_[:, e, :], po[:], modv[:, 32 + 2 * e + gi:32 + 2 * e + gi + 1], h_[:, e, :], ALU.mult, ALU.add,
                         [po, modv, h_], [h_])
            self.dma(hTv[:, :, c0:c0 + 512], h_[:], R=[h_], owner=h_)
        self.end(f"p5b_{l}")

    def build(self):
        seq = [("mod", self.phase_mod), ("in", self.phase_in)]
        for l in range(self.nlayers):
            seq += [(f"p1_{l}", lambda l=l: self.phase_p1(l)),
                    (f"four_{l}", lambda l=l: self.phase_four(l)),
                    (f"four3_{l}", lambda l=l: self.phase_four3(l)),
                    (f"delta_{l}", lambda l=l: self.phase_delta(l)),
                    (f"p5a_{l}", lambda l=l: self.phase_p5a(l)),
                    (f"p5b_{l}", lambda l=l: self.phase_p5b(l))]
        seq += [("out", self.phase_out)]
        for name, fn in seq:
            fn()
            if self.stop_after == name:
                break
        return self.nc


def host_consts():
    i = np.arange(128)
    ident = np.eye(128, dtype=np.float32)
    tri = np.zeros((128, 6, 128), np.float32)
    tri[:, 0, :] = (i[:, None] <= i[None, :])
    tri[:, 1, :] = (i[:, None] >= i[None, :])
    tri[:, 2, :] = (i[:, None] > i[None, :])
    tri[:, 3, :] = (i[:, None] < i[None, :])
    tri[:, 4, :] = ((i[:, None] // 64) == (i[None, :] // 64))
    tri[:, 5, :] = 1.0 - tri[:, 4, :]

    def cs(n):
        a = np.arange(n)
        th = 2 * np.pi * np.outer(a, a) / n
        return (np.cos(th) / np.sqrt(n)), (np.sin(th) / np.sqrt(n))
    c128, s128 = cs(128)
    cs4 = np.concatenate([c128, s128, -s128, c128], axis=1).astype(np.float32)
    c64, s64 = cs(64)
    bd = np.zeros((128, 2, 128), np.float32)
    for r in range(2):
        bd[r * 64:(r + 1) * 64, 0, r * 64:(r + 1) * 64] = c64
        bd[r * 64:(r + 1) * 64, 1, r * 64:(r + 1) * 64] = s64
    csr = np.concatenate([c64, -s64], axis=0).astype(np.float32)
    c256, s256 = cs(256)
    cp = np.zeros((128, 2, 2, 256), np.float32)
    for tc in range(2):
        cp[:, 0, tc, :] = c256[tc * 128:(tc + 1) * 128, :]
        cp[:, 1, tc, :] = -s256[tc * 128:(tc + 1) * 128, :]
    return dict(c_ident=ident, c_tri=tri, c_cs4=cs4, c_bd=bd, c_csr=csr, c_cp=cp)


def make_in_maps(x_prompt, x_sample, state_delta, c, c_ctx, ln_g, w_ada, b_ada, w_in, conv_w, a_log,
                 dt_bias, o_norm_g, sgu_norm_g, w_spatial, b_spatial, w_branch, w_out, final_g):
    f = lambda a: np.ascontiguousarray(np.asarray(a, dtype=np.float32))
    consts = host_consts()
    shared = dict(
        ln_g=f(ln_g).reshape(32, 128), w_ada=f(w_ada), b_ada=f(b_ada).reshape(DEPTH, 24, 128), w_in=f(w_in),
        conv_w=f(conv_w).reshape(DEPTH, 36, 128), a_log=f(a_log).reshape(DEPTH, 8), dt_bias=f(dt_bias).reshape(DEPTH, 8),
        o_norm_g=f(o_norm_g), sgu_g=f(sgu_norm_g), w_sp=f(w_spatial), b_sp=f(b_spatial).reshape(DEPTH, 512),
        w_br=f(w_branch).reshape(DEPTH, 1536, D), w_out=f(w_out), final_g=f(final_g), **consts)
    x_prompt = f(x_prompt); x_sample = f(x_sample); state_delta = f(state_delta); c = f(c); c_ctx = f(c_ctx)
    maps = []
    for i in range(8):
        m = dict(shared)
        m["x_s"] = x_sample[i]
        m["x_p"] = np.ascontiguousarray(x_prompt[2 * i:2 * i + 2].reshape(NP, D))
        m["state"] = state_delta[i]
        m["c2"] = np.ascontiguousarray(np.stack([c[i], c_ctx], 0).reshape(16, 128))
        maps.append(m)
    return maps


def kernel(**inputs):
    nc = K().build()
    maps = make_in_maps(**inputs)
    res = run_bass_kernel_spmd(nc, maps, core_ids=list(range(8)))
    r = res.results
    y_sample = np.stack([r[i]["y_s"] for i in range(8)], 0)
    y_prompt = np.concatenate([r[i]["y_p"].reshape(2, 256, D) for i in range(8)], 0)
    ns = np.concatenate([r[i]["ns"] for i in range(8)], 0)
    return (y_prompt.astype(np.float32), y_sample.astype(np.float32), ns.astype(np.float32))
```
